# Optimizing a Trainium2 kernel written in Bass

```python
import math
import jax, jax.numpy as jnp
from jax import lax
import numpy as np

D_MODEL = 1024
BATCH = 8
SEQ = 4096
DEPTH = 1

N_Q_HEADS = 8
HEAD_DIM = 64
N_KV_GROUPS = 2
ATTN_WIDTH = N_Q_HEADS * HEAD_DIM
CMP_BLOCK = 32
CMP_STRIDE = 16
CMP_HIDDEN = 256
SEL_BLOCK = 64
SEL_TOPN = 16
WINDOW = 512
Q_BLOCK = 128
LRU_WIDTH = D_MODEL - ATTN_WIDTH
LRU_BLOCKS = 8
LRU_BLOCK_DIM = LRU_WIDTH // LRU_BLOCKS
CONV_WIDTH = 4
LRU_C = 8.0
IN_WIDTH = ATTN_WIDTH + 6 * N_KV_GROUPS * HEAD_DIM + 3 * N_Q_HEADS + 2 * LRU_WIDTH
D_FF = 2816
NORM_EPS = 1e-6

kernel_name = 'hymba_nsa_rglru_macaron_sandwich'


def rms_norm(x, g):
    xf = x.astype(jnp.float32)
    y = xf * lax.rsqrt(jnp.mean(xf * xf, axis=-1, keepdims=True) + NORM_EPS)
    return (y * g.astype(jnp.float32)).astype(x.dtype)


def swiglu(x, w_gate, w_up, w_down):
    return (jax.nn.silu(x @ w_gate) * (x @ w_up)) @ w_down


def alibi_slopes(n):
    return jnp.asarray(np.power(2.0, -8.0 * np.arange(1, n + 1) / n), dtype=jnp.float32)


def masked_softmax(s, mask):
    s = jnp.where(mask, s, -jnp.inf)
    m = jnp.max(s, axis=-1, keepdims=True)
    m = jnp.where(jnp.isfinite(m), m, 0.0)
    p = jnp.where(mask, jnp.exp(s - m), 0.0)
    return p / jnp.maximum(jnp.sum(p, axis=-1, keepdims=True), 1e-30)


def compress_blocks(t, pe, w1, w2):
    B, T, G, dh = t.shape
    nc = (T - CMP_BLOCK) // CMP_STRIDE + 1
    idx = jnp.arange(nc)[:, None] * CMP_STRIDE + jnp.arange(CMP_BLOCK)[None, :]
    blk = t[:, idx] + pe[None, None, :, None, :]
    blk = blk.transpose(0, 3, 1, 2, 4).reshape(B, G, nc, CMP_BLOCK * dh)
    return jax.nn.gelu(blk @ w1) @ w2


def cmp_to_sel_matrix(n_cmp, n_sel):
    cs = (jnp.arange(n_cmp) * CMP_STRIDE)[:, None]
    js = (jnp.arange(n_sel) * SEL_BLOCK)[None, :]
    ov = jnp.minimum(cs + CMP_BLOCK, js + SEL_BLOCK) - jnp.maximum(cs, js)
    return jnp.maximum(ov, 0).astype(jnp.float32) / CMP_BLOCK


def nsa_attention(q, k_cmp, v_cmp, k_sel, v_sel, k_win, v_win, gates):
    B, T, G, R, dh = q.shape
    NC = k_cmp.shape[2]
    NS = k_sel.shape[2]
    NQB = T // Q_BLOCK
    n_top = min(SEL_TOPN, NS)
    scale = dh ** -0.5
    slopes = alibi_slopes(G * R).reshape(G, R)[None, :, :, None, None]
    cmp_end = jnp.arange(NC) * CMP_STRIDE + (CMP_BLOCK - 1)
    m_cs = cmp_to_sel_matrix(NC, NS)
    pad = ((0, 0), (0, 0), (WINDOW, 0), (0, 0))
    k_win_p = jnp.pad(k_win, pad)
    v_win_p = jnp.pad(v_win, pad)
    b_idx = jnp.arange(B)[:, None, None, None]
    g_idx = jnp.arange(G)[None, :, None, None]
    blk_ids = jnp.arange(NS)
    qb = q.reshape(B, NQB, Q_BLOCK, G, R, dh).transpose(1, 0, 3, 4, 2, 5)
    gb = gates.reshape(B, NQB, Q_BLOCK, G, R, 3).transpose(1, 0, 3, 4, 2, 5)
    q0s = jnp.arange(NQB, dtype=jnp.int32) * Q_BLOCK

    def block(args):
        qi, gq, q0 = args
        t = q0 + jnp.arange(Q_BLOCK)
        s = jnp.einsum('bgrqd,bgcd->bgrqc', qi, k_cmp).astype(jnp.float32) * scale
        dist = t[:, None] - cmp_end[None, :]
        s = s - slopes * dist.astype(jnp.float32)
        p_cmp = masked_softmax(s, dist >= 0)
        o_cmp = jnp.einsum('bgrqc,bgcd->bgrqd', p_cmp.astype(v_cmp.dtype), v_cmp)
        imp = jnp.einsum('bgrqc,cs->bgqs', p_cmp, m_cs)
        cur = (t // SEL_BLOCK)[:, None]
        valid = blk_ids[None, :] * SEL_BLOCK <= t[:, None]
        forced = (blk_ids[None, :] == 0) | (blk_ids[None, :] == cur) | (blk_ids[None, :] == cur - 1)
        imp = jnp.where(forced, jnp.inf, jnp.where(valid, imp, -jnp.inf))
        _, sel = lax.top_k(imp, n_top)
        kg = k_sel[b_idx, g_idx, sel]
        vg = v_sel[b_idx, g_idx, sel]
        pos = sel[..., None] * SEL_BLOCK + jnp.arange(SEL_BLOCK)
        dist = (t[None, None, :, None, None] - pos)[:, :, None]
        s = jnp.einsum('bgrqd,bgqnkd->bgrqnk', qi, kg).astype(jnp.float32) * scale
        s = s - slopes[..., None] * dist.astype(jnp.float32)
        flat = n_top * SEL_BLOCK
        p = masked_softmax(s.reshape(B, G, R, Q_BLOCK, flat),
                           (dist >= 0).reshape(B, G, 1, Q_BLOCK, flat)).reshape(s.shape)
        o_sel = jnp.einsum('bgrqnk,bgqnkd->bgrqd', p.astype(vg.dtype), vg)
        kw = lax.dynamic_slice_in_dim(k_win_p, q0, Q_BLOCK + WINDOW, axis=2)
        vw = lax.dynamic_slice_in_dim(v_win_p, q0, Q_BLOCK + WINDOW, axis=2)
        spos = q0 - WINDOW + jnp.arange(Q_BLOCK + WINDOW)
        dist = t[:, None] - spos[None, :]
        mask = (dist >= 0) & (dist < WINDOW) & (spos >= 0)[None, :]
        s = jnp.einsum('bgrqd,bgkd->bgrqk', qi, kw).astype(jnp.float32) * scale
        s = s - slopes * dist.astype(jnp.float32)
        p = masked_softmax(s, mask)
        o_win = jnp.einsum('bgrqk,bgkd->bgrqd', p.astype(vw.dtype), vw)
        g = gq.astype(jnp.float32)
        out = (g[..., 0:1] * o_cmp.astype(jnp.float32) + g[..., 1:2] * o_sel.astype(jnp.float32)
               + g[..., 2:3] * o_win.astype(jnp.float32))
        return out.astype(q.dtype)

    ob = lax.map(block, (qb, gb, q0s))
    return ob.transpose(1, 0, 4, 2, 3, 5).reshape(B, T, G, R, dh)


def _lru_combine(c1, c2):
    a1, b1 = c1
    a2, b2 = c2
    return a1 * a2, a2 * b1 + b2


def rg_lru(xr, xg, conv_w, conv_b, w_a, b_a, w_x, b_x, lam):
    B, T, C = xr.shape
    xc = lax.conv_general_dilated(xr, conv_w[:, None, :].astype(xr.dtype), window_strides=(1,),
                                  padding=[(CONV_WIDTH - 1, 0)],
                                  dimension_numbers=('NWC', 'WIO', 'NWC'),
                                  feature_group_count=C) + conv_b
    xb = xc.reshape(B, T, LRU_BLOCKS, LRU_BLOCK_DIM)
    r = jax.nn.sigmoid((jnp.einsum('btnd,nde->btne', xb, w_a).reshape(B, T, C) + b_a).astype(jnp.float32))
    i = jax.nn.sigmoid((jnp.einsum('btnd,nde->btne', xb, w_x).reshape(B, T, C) + b_x).astype(jnp.float32))
    log_a = -LRU_C * r * jax.nn.softplus(-lam.astype(jnp.float32))
    a = jnp.exp(log_a)
    mult = jnp.sqrt(jnp.maximum(-jnp.expm1(2.0 * log_a), 0.0))
    b = mult * i * xc.astype(jnp.float32)
    _, hs = lax.associative_scan(_lru_combine, (a, b), axis=1)
    return (hs * jax.nn.gelu(xg.astype(jnp.float32))).astype(xr.dtype)


def hybrid_mixer(h, w_in, cmp_k_pe, cmp_k_w1, cmp_k_w2, cmp_v_pe, cmp_v_w1, cmp_v_w2,
                 conv_w, conv_b, lru_w_a, lru_b_a, lru_w_x, lru_b_x, lru_lambda,
                 attn_out_g, lru_out_g, w_out):
    B, T, _ = h.shape
    G, dh = N_KV_GROUPS, HEAD_DIM
    R = N_Q_HEADS // G
    kv = G * dh
    sizes = [ATTN_WIDTH, kv, kv, kv, kv, kv, kv, 3 * N_Q_HEADS, LRU_WIDTH, LRU_WIDTH]
    cuts = np.cumsum(sizes)[:-1].tolist()
    q, kc, vc, ks, vs, kw, vw, gl, xr, xg = jnp.split(h @ w_in, cuts, axis=-1)
    q = q.reshape(B, T, G, R, dh)
    k_cmp = compress_blocks(kc.reshape(B, T, G, dh), cmp_k_pe, cmp_k_w1, cmp_k_w2)
    v_cmp = compress_blocks(vc.reshape(B, T, G, dh), cmp_v_pe, cmp_v_w1, cmp_v_w2)
    NS = T // SEL_BLOCK
    k_sel = ks.reshape(B, NS, SEL_BLOCK, G, dh).transpose(0, 3, 1, 2, 4)
    v_sel = vs.reshape(B, NS, SEL_BLOCK, G, dh).transpose(0, 3, 1, 2, 4)
    k_win = kw.reshape(B, T, G, dh).transpose(0, 2, 1, 3)
    v_win = vw.reshape(B, T, G, dh).transpose(0, 2, 1, 3)
    gates = jax.nn.sigmoid(gl.astype(jnp.float32)).reshape(B, T, G, R, 3)
    attn = nsa_attention(q, k_cmp, v_cmp, k_sel, v_sel, k_win, v_win, gates).reshape(B, T, ATTN_WIDTH)
    lru = rg_lru(xr, xg, conv_w, conv_b, lru_w_a, lru_b_a, lru_w_x, lru_b_x, lru_lambda)
    y = jnp.concatenate([rms_norm(attn, attn_out_g), rms_norm(lru, lru_out_g)], axis=-1)
    return y @ w_out


def setup_inputs(seed: int = 0) -> dict:
    key = jax.random.key(seed)
    keys = iter(jax.random.split(key, 40))

    def nrm(shape, scale):
        return jax.random.normal(next(keys), shape, jnp.float32) * scale

    def gain(n):
        return 1.0 + nrm((DEPTH, n), 0.02)

    D, L = D_MODEL, DEPTH
    u = jax.random.uniform(next(keys), (L, LRU_WIDTH), jnp.float32, minval=0.9, maxval=0.999)
    sig = u ** (1.0 / LRU_C)
    lru_lambda = jnp.log(sig) - jnp.log1p(-sig)
    return {
        'x': nrm((BATCH, SEQ, D), 1.0),
        'ffn1_pre_g': gain(D), 'ffn1_post_g': gain(D),
        'ffn1_w_gate': nrm((L, D, D_FF), D ** -0.5), 'ffn1_w_up': nrm((L, D, D_FF), D ** -0.5),
        'ffn1_w_down': nrm((L, D_FF, D), D_FF ** -0.5),
        'mix_pre_g': gain(D), 'mix_post_g': gain(D),
        'w_in': nrm((L, D, IN_WIDTH), D ** -0.5),
        'cmp_k_pe': nrm((L, CMP_BLOCK, HEAD_DIM), 0.1),
        'cmp_k_w1': nrm((L, CMP_BLOCK * HEAD_DIM, CMP_HIDDEN), (CMP_BLOCK * HEAD_DIM) ** -0.5),
        'cmp_k_w2': nrm((L, CMP_HIDDEN, HEAD_DIM), CMP_HIDDEN ** -0.5),
        'cmp_v_pe': nrm((L, CMP_BLOCK, HEAD_DIM), 0.1),
        'cmp_v_w1': nrm((L, CMP_BLOCK * HEAD_DIM, CMP_HIDDEN), (CMP_BLOCK * HEAD_DIM) ** -0.5),
        'cmp_v_w2': nrm((L, CMP_HIDDEN, HEAD_DIM), CMP_HIDDEN ** -0.5),
        'conv_w': nrm((L, CONV_WIDTH, LRU_WIDTH), CONV_WIDTH ** -0.5),
        'conv_b': nrm((L, LRU_WIDTH), 0.01),
        'lru_w_a': nrm((L, LRU_BLOCKS, LRU_BLOCK_DIM, LRU_BLOCK_DIM), LRU_BLOCK_DIM ** -0.5),
        'lru_b_a': nrm((L, LRU_WIDTH), 0.01),
        'lru_w_x': nrm((L, LRU_BLOCKS, LRU_BLOCK_DIM, LRU_BLOCK_DIM), LRU_BLOCK_DIM ** -0.5),
        'lru_b_x': nrm((L, LRU_WIDTH), 0.01),
        'lru_lambda': lru_lambda,
        'attn_out_g': gain(ATTN_WIDTH), 'lru_out_g': gain(LRU_WIDTH),
        'w_out': nrm((L, D, D), D ** -0.5),
        'ffn2_pre_g': gain(D), 'ffn2_post_g': gain(D),
        'ffn2_w_gate': nrm((L, D, D_FF), D ** -0.5), 'ffn2_w_up': nrm((L, D, D_FF), D ** -0.5),
        'ffn2_w_down': nrm((L, D_FF, D), D_FF ** -0.5),
    }


def reference(x, ffn1_pre_g, ffn1_post_g, ffn1_w_gate, ffn1_w_up, ffn1_w_down,
              mix_pre_g, mix_post_g, w_in, cmp_k_pe, cmp_k_w1, cmp_k_w2,
              cmp_v_pe, cmp_v_w1, cmp_v_w2, conv_w, conv_b, lru_w_a, lru_b_a,
              lru_w_x, lru_b_x, lru_lambda, attn_out_g, lru_out_g, w_out,
              ffn2_pre_g, ffn2_post_g, ffn2_w_gate, ffn2_w_up, ffn2_w_down):
    h = x
    for l in range(DEPTH):
        f1 = swiglu(rms_norm(h, ffn1_pre_g[l]), ffn1_w_gate[l], ffn1_w_up[l], ffn1_w_down[l])
        h = h + 0.5 * rms_norm(f1, ffn1_post_g[l])
        m = hybrid_mixer(rms_norm(h, mix_pre_g[l]), w_in[l], cmp_k_pe[l], cmp_k_w1[l], cmp_k_w2[l],
                         cmp_v_pe[l], cmp_v_w1[l], cmp_v_w2[l], conv_w[l], conv_b[l],
                         lru_w_a[l], lru_b_a[l], lru_w_x[l], lru_b_x[l], lru_lambda[l],
                         attn_out_g[l], lru_out_g[l], w_out[l])
        h = h + rms_norm(m, mix_post_g[l])
        f2 = swiglu(rms_norm(h, ffn2_pre_g[l]), ffn2_w_gate[l], ffn2_w_up[l], ffn2_w_down[l])
        h = h + 0.5 * rms_norm(f2, ffn2_post_g[l])
    return h
```

```python
import contextlib
import numpy as np
import concourse.bass as bass
import concourse.mybir as mybir
from concourse.bass_utils import run_bass_kernel_spmd

F32 = mybir.dt.float32
BF16 = mybir.dt.bfloat16
AF = mybir.ActivationFunctionType
ALU = mybir.AluOpType
AX = mybir.AxisListType

D = 1024
DFF = 2816
NF = DFF // 128
MT = 512
NQH = 8
DH = 64
NG = 2
INW = 2328
EPS = 1e-6
BIG = 30000.0
O_Q, O_KC, O_VC, O_KS, O_VS, O_KW, O_VW, O_GL, O_XR, O_XG = 0, 512, 640, 768, 896, 1024, 1152, 1280, 1304, 1816


class Op:
    __slots__ = ("eng", "fn", "deps", "dma", "sig", "tok", "idx")

    def __init__(self, eng, fn, deps, dma):
        self.eng = eng
        self.fn = fn
        self.deps = deps
        self.dma = dma
        self.sig = False
        self.tok = None


class Sched:
    ENGS = ("pe", "act", "dve", "pool", "sp")
    EPOCH = 24000
    DMA_POOL = {"sp": 8, "pool": 4, "act": 4}

    def __init__(self):
        self.q = {e: [] for e in self.ENGS}
        self.lastw = {}
        self.readers = {}
        self.all_ops = []
        self.dma_hist = {e: [] for e in self.DMA_POOL}

    def add(self, eng, fn, reads=(), writes=(), dma=False):
        deps = []
        for k in reads:
            w = self.lastw.get(k)
            if w is not None:
                deps.append(w)
            if isinstance(k, tuple) and k[0] == "ps":
                for r in self.readers.get(k, ()):
                    if r.eng != eng:
                        deps.append(r)
        for k in writes:
            w = self.lastw.get(k)
            if w is not None:
                deps.append(w)
            deps.extend(self.readers.get(k, ()))
        op = Op(eng, fn, None, dma)
        if dma:
            hist = self.dma_hist[eng]
            P = self.DMA_POOL[eng]
            if len(hist) >= P:
                deps.append(hist[len(hist) - P])
            hist.append(op)
        dd = []
        seen = set()
        for d in deps:
            if id(d) in seen:
                continue
            seen.add(id(d))
            if (not dma) and (not d.dma) and d.eng == "pe" and eng == "pe":
                continue
            dd.append(d)
            d.sig = True
        op.deps = dd
        for k in reads:
            self.readers.setdefault(k, []).append(op)
        for k in writes:
            self.lastw[k] = op
            self.readers[k] = []
        self.q[eng].append(op)
        self.all_ops.append(op)
        return op

    def emit(self, nc, block, engines, final_waits):
        n_epochs = {}
        for e in self.ENGS:
            n = sum(1 for o in self.q[e] if (o.sig and not o.dma))
            n_epochs[e] = max(1, (n + self.EPOCH - 1) // self.EPOCH)
        stack = contextlib.ExitStack()
        sems = {}
        for e in self.ENGS:
            sems[e] = [stack.enter_context(nc.semaphore("s_%s_%d" % (e, i))) for i in range(n_epochs[e])]
        dsems = {}
        for e, P in self.DMA_POOL.items():
            if self.dma_hist[e]:
                dsems[e] = [stack.enter_context(nc.semaphore("d_%s_%d" % (e, i))) for i in range(min(P, len(self.dma_hist[e])))]
        for e in self.ENGS:
            c = 0
            for o in self.q[e]:
                if o.dma:
                    continue
                if o.sig:
                    o.tok = (sems[e][c // self.EPOCH], c % self.EPOCH + 1)
                    c += 1
        for e, hist in self.dma_hist.items():
            P = self.DMA_POOL[e]
            for j, o in enumerate(hist):
                o.tok = (dsems[e][j % P], 16 * (j // P + 1))
        fw = [o.tok for o in final_waits]

        def run_engine(ename):
            def body(eng):
                waited = {}
                for o in self.q[ename]:
                    for d in o.deps:
                        s, v = d.tok
                        if waited.get(id(s), 0) >= v:
                            continue
                        eng.wait_ge(s, v)
                        waited[id(s)] = v
                    ins = o.fn(eng)
                    if o.dma:
                        ins.then_inc(o.tok[0], 16)
                    elif o.sig:
                        ins.then_inc(o.tok[0], 1)
                if ename == "sp":
                    for s, v in fw:
                        eng.wait_ge(s, v)
            return body

        block.tensor(run_engine("pe"))
        block.scalar(run_engine("act"))
        block.vector(run_engine("dve"))
        block.gpsimd(run_engine("pool"))
        block.sync(run_engine("sp"))
        return stack


def unit_directory():
    names = []
    for pre in ("1", "2"):
        names += ["G%s_%d" % (pre, f) for f in range(NF)]
        names += ["U%s_%d" % (pre, f) for f in range(NF)]
        names += ["D%s_%d" % (pre, f) for f in range(NF)]
    names += ["IN_%d" % u for u in range(21)]
    names += ["C1K_%d" % u for u in range(4)] + ["C1V_%d" % u for u in range(4)]
    names += ["WO_%d" % u for u in range(8)]
    return {n: i for i, n in enumerate(names)}


UNITS = unit_directory()
NUNITS = len(UNITS)


def macro_units(m_first):
    seq = []
    for pre in ("1",):
        for f in range(NF):
            seq += ["G1_%d" % f, "U1_%d" % f]
        seq += ["D1_%d" % f for f in range(NF)]
    seq += ["IN_%d" % u for u in range(21)]
    seq += ["C1K_%d" % u for u in range(4)] + ["C1V_%d" % u for u in range(4)]
    for s in range(4):
        seq += ["WO_0", "WO_1", "WO_2", "WO_3", "WO_4", "WO_5", "WO_6", "WO_7"]
    for f in range(NF):
        seq += ["G2_%d" % f, "U2_%d" % f]
    seq += ["D2_%d" % f for f in range(NF)]
    return seq


def host_consts(T):
    import ml_dtypes
    bf = ml_dtypes.bfloat16
    pos = np.arange(T)
    kaug = np.stack([pos // 128, pos % 128, np.ones(T)]).astype(np.float32)
    slopes = np.power(2.0, -np.arange(1, 9)).astype(np.float64)
    ntile = T // 128
    qaug = np.zeros((T // MT, 3, 8, MT), np.float32)
    for i in range(ntile):
        m, s = divmod(i, 4)
        qaug[m, 0, :, s * 128:(s + 1) * 128] = (128.0 * slopes)[:, None]
        qaug[m, 1, :, s * 128:(s + 1) * 128] = slopes[:, None]
        qaug[m, 2, :, s * 128:(s + 1) * 128] = (-slopes * 128.0 * i)[:, None]
    ncp = T // 16
    cpos = 16 * np.arange(ncp) + 15
    kcaug = np.stack([cpos // 128, cpos % 128, np.ones(ncp)]).astype(np.float32)
    kcaug[:, 0] = 0.0
    ncpad = ((ncp + 127) // 128) * 128
    mcs = np.zeros((ncpad, 64), np.float32)
    for cp in range(1, ncp):
        c = cp - 1
        for j in range(64):
            ov = min(16 * c + 32, 64 * j + 64) - max(16 * c, 64 * j)
            if ov > 0:
                mcs[cp, j] = ov / 32.0
    expand = np.zeros((128, T), np.float32)
    for k in range(T):
        expand[k // 64, k] = 1.0 if k // 64 < 64 else 0.0
    kk = np.arange(128)[:, None]
    qq = np.arange(128)[None, :]
    causal = np.where(kk > qq, -BIG, 0.0).astype(np.float32)
    winm = np.where(kk <= qq, -BIG, 0.0).astype(np.float32)
    causalb = np.tile(causal, (1, 4))
    winb = np.tile(winm, (1, 4))
    ident = np.eye(128, dtype=np.float32)
    fpat = np.zeros((128, 3), np.float32)
    fpat[:64, 0] = 1e9
    fpat[:64, 1] = 1e9
    fpat[64:, 1] = 1e9
    fpat[64:, 2] = 1e9
    c = {
        "c_kaug": kaug.astype(bf), "c_qaug": qaug.reshape(T // MT, 3, 8 * MT).astype(bf),
        "c_kcaug": kcaug.astype(bf), "c_mcs": mcs.astype(bf), "c_expand": expand.astype(bf),
        "c_causalb": causalb.astype(bf), "c_winb": winb.astype(bf), "c_ident": ident.astype(bf),
        "c_fpat": fpat, "c_identf": np.eye(128, dtype=np.float32),
    }
    return c


WNAMES = ["ffn1_pre_g", "ffn1_post_g", "ffn1_w_gate", "ffn1_w_up", "ffn1_w_down", "mix_pre_g", "mix_post_g",
          "w_in", "cmp_k_pe", "cmp_k_w1", "cmp_k_w2", "cmp_v_pe", "cmp_v_w1", "cmp_v_w2", "conv_w", "conv_b",
          "lru_w_a", "lru_b_a", "lru_w_x", "lru_b_x", "lru_lambda", "attn_out_g", "lru_out_g", "w_out",
          "ffn2_pre_g", "ffn2_post_g", "ffn2_w_gate", "ffn2_w_up", "ffn2_w_down"]
WSHAPES = {
    "ffn1_pre_g": [1, D], "ffn1_post_g": [1, D], "ffn1_w_gate": [1, D, DFF], "ffn1_w_up": [1, D, DFF],
    "ffn1_w_down": [1, DFF, D], "mix_pre_g": [1, D], "mix_post_g": [1, D], "w_in": [1, D, INW],
    "cmp_k_pe": [1, 32, 64], "cmp_k_w1": [1, 2048, 256], "cmp_k_w2": [1, 256, 64],
    "cmp_v_pe": [1, 32, 64], "cmp_v_w1": [1, 2048, 256], "cmp_v_w2": [1, 256, 64],
    "conv_w": [1, 4, 512], "conv_b": [1, 512], "lru_w_a": [1, 8, 64, 64], "lru_b_a": [1, 512],
    "lru_w_x": [1, 8, 64, 64], "lru_b_x": [1, 512], "lru_lambda": [1, 512], "attn_out_g": [1, 512],
    "lru_out_g": [1, 512], "w_out": [1, D, D], "ffn2_pre_g": [1, D], "ffn2_post_g": [1, D],
    "ffn2_w_gate": [1, D, DFF], "ffn2_w_up": [1, D, DFF], "ffn2_w_down": [1, DFF, D],
}


def build(T, stage=9, NS=10, wseq=None):
    NM = T // MT
    NT = T // 128
    NCP = T // 16
    NCC = (NCP + 127) // 128
    consts = host_consts(T)
    nc = bass.Bass("TRN2", target_bir_lowering=False)
    dr = {}
    dr["x"] = nc.dram_tensor("x", [T, D], F32, kind="ExternalInput").ap()
    for n in WNAMES:
        dr[n] = nc.dram_tensor(n, WSHAPES[n], F32, kind="ExternalInput").ap()
    for n, a in consts.items():
        dr[n] = nc.dram_tensor(n, list(a.shape), F32 if a.dtype == np.float32 else BF16, kind="ExternalInput").ap()
    y = nc.dram_tensor("y", [T, D], F32, kind="ExternalOutput").ap()
    wscr = nc.dram_tensor("wscr", [NUNITS, 128, 1024], BF16, kind="Internal").ap()

    S = Sched()
    es = contextlib.ExitStack()

    def sb(name, shape, dt):
        return es.enter_context(nc.sbuf_tensor(name, shape, dt))

    def A(eng, meth, reads, writes, *args, **kw):
        dma = kw.pop("_dma", False)
        return S.add(eng, lambda e: getattr(e, meth)(*args, **kw), reads, writes, dma=dma)

    def DMA(eng, reads, writes, out, in_, slow=False):
        if slow:
            return S.add(eng, lambda e: e.dma_start(out=out, in_=in_, allow_slow_non_contiguous=True), reads, writes, dma=True)
        return S.add(eng, lambda e: e.dma_start(out=out, in_=in_), reads, writes, dma=True)

    xb = sb("xb", [128, 8, 1024], F32)
    xn = sb("xn", [128, 2, 1024], BF16)
    xnT = sb("xnT", [128, 8, MT], BF16)
    hT = sb("hT", [128, NF, MT], BF16)
    ring = sb("ring", [128, NS, 1024], BF16)
    gpost = sb("gpost", [128, 3, 1024], F32)
    ident = sb("ident", [128, 128], BF16)
    small = sb("small", [128, 80], F32)
    junk = sb("junk", [128, 1, 1024], BF16)
    sg = sb("sg", [128, 2, MT], F32)
    tmpf = sb("tmpf", [128, 2, 512], F32)
    ps = [es.enter_context(nc.psum_tensor("ps%d" % b, [128, 512], F32)) for b in range(8)]

    st_f32 = hT[:].rearrange("p a b -> p (a b)").bitcast(F32)
    st_bf = xb[:].rearrange("p a b -> p (a b)").bitcast(BF16)

    fin = []

    DMA("sp", [], ["ident"], ident[:], dr["c_ident"])
    identf = sb("identf", [128, 128], F32)
    DMA("sp", [], ["identf"], identf[:], dr["c_identf"])
    pk = sb("pk", [64, 128], F32)
    colT = sb("colT", [128, 64], F32)
    for j, n in enumerate(["ffn1_pre_g", "mix_pre_g", "ffn2_pre_g"]):
        DMA("sp", [], [("pk", j)], pk[8 * j:8 * j + 8, :], dr[n][0].rearrange("(k p) -> k p", p=128))
    DMA("sp", [], [("pk", 3)], pk[24:28, :], dr["attn_out_g"][0].rearrange("(k p) -> k p", p=128))
    DMA("sp", [], [("pk", 4)], pk[28:32, :], dr["lru_out_g"][0].rearrange("(k p) -> k p", p=128))
    DMA("sp", [], [("pk", 5)], pk[32:48, :], dr["conv_w"][0].rearrange("j (c p) -> (j c) p", p=128))
    for j, n in enumerate(["conv_b", "lru_b_a", "lru_b_x", "lru_lambda"]):
        DMA("sp", [], [("pk", 6 + j)], pk[48 + 4 * j:52 + 4 * j, :], dr[n][0].rearrange("(c p) -> c p", p=128))
    A("pe", "transpose", [("pk", j) for j in range(10)] + ["identf"], [("ps", 0)], out=ps[0][:, 0:64], in_=pk[:], identity=identf[0:64, 0:64])
    A("dve", "tensor_copy", [("ps", 0)], ["gcol", "gocol", "lcol0"], out=colT[:], in_=ps[0][:, 0:64])
    gcol = colT[:, 0:24].rearrange("p (j k) -> p j k", k=8)
    gocol = colT[:, 24:32]
    for j, n in enumerate(["ffn1_post_g", "mix_post_g", "ffn2_post_g"]):
        DMA("sp", [], [("gpost", j)], gpost[:, j, :], dr[n][0:1, :].partition_broadcast(128))
    for j in (0, 2):
        A("pool", "tensor_scalar", [("gpost", j)], [("gpost", j)], out=gpost[:, j, :], in0=gpost[:, j, :], scalar1=0.5,
          scalar2=None, op0=ALU.mult)

    cast_rr = [0]

    def cast(out, in_, scale, reads, writes):
        e = ("dve", "act")[cast_rr[0] % 2]
        cast_rr[0] += 1
        if e == "act":
            if scale is None:
                A("act", "activation", reads, writes, out=out, in_=in_, func=AF.Copy)
            else:
                A("act", "activation", reads, writes, out=out, in_=in_, func=AF.Copy, scale=scale)
        else:
            if scale is None:
                A(e, "tensor_copy", reads, writes, out=out, in_=in_)
            else:
                A(e, "tensor_scalar", reads, writes, out=out, in0=in_, scalar1=scale, scalar2=None, op0=ALU.mult)

    ld_rr = [0]

    def stage_load(src, width):
        j = ld_rr[0] % 2
        ld_rr[0] += 1
        v = st_f32[:, j * 2816: j * 2816 + width]
        DMA("sp", [], [("stf", j)], v, src)
        return v, ("stf", j)

    STB = [("stb", j) for j in range(24)]

    def flush(unit_names, nel):
        for i, un in enumerate(unit_names):
            DMA("act" if i % 2 else "sp", STB, [("wscr", un)], wscr[UNITS[un]], st_bf[:, i * 1024:(i + 1) * 1024])

    def conv_gate_like(wname, gidx, prefix):
        for fh in range(2):
            stv = st_bf[:, 0:11 * 1024].rearrange("p (f k c) -> p f k c", f=11, k=8, c=128)
            for k in range(8):
                v, key = stage_load(dr[wname][0, k * 128:(k + 1) * 128, fh * 1408:(fh + 1) * 1408], 1408)
                cast(stv[:, :, k, :], v.rearrange("p (f c) -> p f c", c=128), gcol[:, gidx, k:k + 1], [key, "gcol"], [("stb", k)])
            flush(["%s_%d" % (prefix, fh * 11 + f) for f in range(11)], 11 * 1024)

    def conv_down(wname, prefix):
        for fh in range(2):
            for f in range(11):
                v, key = stage_load(dr[wname][0, (fh * 11 + f) * 128:(fh * 11 + f + 1) * 128, :], 1024)
                cast(st_bf[:, f * 1024:(f + 1) * 1024], v, None, [key], [("stb", f)])
            flush(["%s_%d" % (prefix, fh * 11 + f) for f in range(11)], 11 * 1024)

    def conv_win():
        stv = st_bf[:, 0:11 * 1024]
        wsA = stv[:, 0:10 * 1024].rearrange("p (u k c) -> p u k c", u=10, k=8, c=128)
        for k in range(8):
            v, key = stage_load(dr["w_in"][0, k * 128:(k + 1) * 128, 0:1152], 1152)
            sc = gcol[:, 1, k:k + 1]
            R, W = [key, "gcol"], [("stb", k)]
            cast(wsA[:, 0:4, k, :], v[:, O_Q:O_Q + 512].rearrange("p (u c) -> p u c", c=128), sc, R, W)
            for kv, off in ((0, O_KC), (1, O_VC)):
                for g in range(2):
                    for h2 in range(2):
                        cast(wsA[:, 4 + 2 * kv + g, k, h2 * 64:(h2 + 1) * 64], v[:, off + g * 64: off + (g + 1) * 64], sc, R, W)
            cast(wsA[:, 8, k, :], v[:, O_KS:O_KS + 128], sc, R, W)
            cast(wsA[:, 9, k, :], v[:, O_KW:O_KW + 128], sc, R, W)
        flush(["IN_%d" % u for u in range(10)], 10 * 1024)
        wsB = stv[:, 0:8 * 1024].rearrange("p (u k c) -> p u k c", u=8, k=8, c=128)
        o0 = O_VS
        for k in range(8):
            v, key = stage_load(dr["w_in"][0, k * 128:(k + 1) * 128, o0:INW], INW - o0)
            sc = gcol[:, 1, k:k + 1]
            R, W = [key, "gcol"], [("stb", k)]
            cast(wsB[:, 0:4, k, :], v[:, O_XR - o0:O_XR - o0 + 512].rearrange("p (u c) -> p u c", c=128), sc, R, W)
            cast(wsB[:, 4:8, k, :], v[:, O_XG - o0:O_XG - o0 + 512].rearrange("p (u c) -> p u c", c=128), sc, R, W)
            u3, kk = divmod(k, 3)
            base = (8 + u3) * 1024 + kk * 280
            cast(stv[:, base:base + 128], v[:, O_VS - o0:O_VS - o0 + 128], sc, R, W)
            cast(stv[:, base + 128:base + 256], v[:, O_VW - o0:O_VW - o0 + 128], sc, R, W)
            cast(stv[:, base + 256:base + 280], v[:, O_GL - o0:O_GL - o0 + 24], sc, R, W)
        flush(["IN_%d" % u for u in range(10, 21)], 11 * 1024)

    def conv_c1(wname, prefix):
        for hf in range(2):
            v, key = stage_load(dr[wname][0, hf * 1024:(hf + 1) * 1024, :].rearrange("(l p) j -> p l j", p=128), 2048)
            cast(st_bf[:, hf * 2048:(hf + 1) * 2048], v, None, [key], [("stb", hf)])
        flush(["%s_%d" % (prefix, u) for u in range(4)], 4 * 1024)

    def conv_wo():
        for k in range(8):
            v, key = stage_load(dr["w_out"][0, k * 128:(k + 1) * 128, :], 1024)
            kp, kq = divmod(k, 2)
            dst = st_bf[:, 0:8 * 1024].rearrange("p (n kp kq c) -> p n kp kq c", n=2, kp=4, kq=2, c=512)[:, :, kp, kq, :]
            cast(dst, v.rearrange("p (n c) -> p n c", c=512), gocol[:, k:k + 1], [key, "gocol"], [("stb", k)])
        flush(["WO_%d" % u for u in range(8)], 8 * 1024)

    conv_gate_like("ffn1_w_gate", 0, "G1")
    conv_gate_like("ffn1_w_up", 0, "U1")
    conv_down("ffn1_w_down", "D1")
    conv_win()
    conv_c1("cmp_k_w1", "C1K")
    conv_c1("cmp_v_w1", "C1V")
    conv_wo()
    conv_gate_like("ffn2_w_gate", 2, "G2")
    conv_gate_like("ffn2_w_up", 2, "U2")
    conv_down("ffn2_w_down", "D2")

    bar_keys = [("xb", j) for j in range(8)] + [("xbh", j) for j in range(8)] + [("hT", f) for f in range(NF)]
    A("sp", "nop", [], STB + [("stf", 0), ("stf", 1)] + bar_keys)

    def XB(slot):
        return [("xb", slot), ("xbh", slot)]

    def XBH(slot, n):
        return ("xb", slot) if n == 0 else ("xbh", slot)

    recording = wseq is None
    rec = []
    if recording:
        wseq = []
    wpos = [0, 0]

    def w_issue():
        if recording:
            return
        i = wpos[0]
        if i >= len(wseq):
            return
        un = wseq[i]
        slot = i % NS
        DMA("sp", [("wscr", un)], [("ring", slot)], ring[:, slot, :], wscr[UNITS[un]])
        wpos[0] += 1

    def w_next(name):
        if recording:
            rec.append(name)
            return ring[:, 0, :], ("ring", 0)
        i = wpos[1]
        assert wseq[i] == name, (wseq[i], name)
        wpos[1] += 1
        slot = i % NS
        return ring[:, slot, :], ("ring", slot)

    for _ in range(NS):
        w_issue()

    sm_rr = [0]

    def smcol():
        c = sm_rr[0] % 40
        sm_rr[0] += 1
        return small[:, c:c + 1], ("small", c)

    epsb = sb("epsb", [128, 2], F32)
    A("pool", "memset", [], ["epsb"], epsb[:, 0:1], EPS)
    A("pool", "memset", [], ["epsb"], epsb[:, 1:2], 1.0)

    def rstd_from(ssq_ap, ssq_key, n):
        t, tk = smcol()
        A("act", "activation", [ssq_key, "epsb"], [tk], out=t, in_=ssq_ap, func=AF.Sqrt, scale=1.0 / n, bias=epsb[:, 0:1])
        r, rk = smcol()
        A("dve", "reciprocal", [tk], [rk], out=r, in_=t)
        return r, rk

    def prenorm_a(gi):
        slot = gi % 8
        j = gi % 2
        q, qk = smcol()
        A("act", "activation", XB(slot), [("junk", 0), qk], out=junk[:, 0, :], in_=xb[:, slot, :], func=AF.Square, accum_out=q)
        r, rk = rstd_from(q, qk, D)
        A("dve", "tensor_scalar", XB(slot) + [rk], [("xn", j)], out=xn[:, j, :], in0=xb[:, slot, :], scalar1=r, scalar2=None, op0=ALU.mult)

    def prenorm_b(gi, s, b=None):
        j = gi % 2
        if b is None:
            b = 7 - (gi % 2)
        pT = ps[b][:].bitcast(BF16)
        for k in range(8):
            A("pe", "transpose", [("xn", j), "ident"], [("ps", b)], out=pT[:, k * 128:(k + 1) * 128], in_=xn[:, j, k * 128:(k + 1) * 128], identity=ident[:])
        if gi % 2:
            A("act", "activation", [("ps", b)], [("xnT", s)], out=xnT[:, :, s * 128:(s + 1) * 128], in_=pT.rearrange("p (k c) -> p k c", c=128), func=AF.Copy)
        else:
            A("dve", "tensor_copy", [("ps", b)], [("xnT", s)], out=xnT[:, :, s * 128:(s + 1) * 128], in_=pT.rearrange("p (k c) -> p k c", c=128))

    def prenorm_T(gi, s, b=None):
        prenorm_a(gi)
        prenorm_b(gi, s, b)

    XNT_ALL = [("xnT", s) for s in range(4)]

    def ffn(m, pre, gidx, after=None, mid=None, first=None):
        if first is not None:
            first()
        for f in range(NF):
            bg, bu = (0, 1) if f % 2 == 0 else (2, 3)
            wg, wgk = w_next("G%s_%d" % (pre, f))
            for k in range(8):
                A("pe", "matmul", [wgk] + XNT_ALL, [("ps", bg)], ps[bg][:], wg[:, k * 128:(k + 1) * 128], xnT[:, k, :], start=(k == 0), stop=(k == 7))
            w_issue()
            wu, wuk = w_next("U%s_%d" % (pre, f))
            for k in range(8):
                A("pe", "matmul", [wuk] + XNT_ALL, [("ps", bu)], ps[bu][:], wu[:, k * 128:(k + 1) * 128], xnT[:, k, :], start=(k == 0), stop=(k == 7))
            w_issue()
            j = f % 2
            A("act", "activation", [("ps", bg)], [("sg", j)], out=sg[:, j, :], in_=ps[bg][:], func=AF.Silu)
            A("dve", "tensor_tensor", [("sg", j), ("ps", bu)], [("hT", f)], out=hT[:, f, :], in0=sg[:, j, :], in1=ps[bu][:], op=ALU.mult)
        if mid is not None:
            mid()
        for f in range(NF):
            wd, wdk = w_next("D%s_%d" % (pre, f))
            for s in range(4):
                for n in range(2):
                    A("pe", "matmul", [wdk, ("hT", f)], [("ps", 2 * s + n)], ps[2 * s + n][:], hT[:, f, s * 128:(s + 1) * 128], wd[:, n * 512:(n + 1) * 512],
                      start=(f == 0), stop=(f == NF - 1))
            w_issue()
        for s in range(4):
            gi = 4 * m + s
            slot = gi % 8
            q, qk = smcol()
            q2, q2k = smcol()
            for n, (qq, qqk) in enumerate(((q, qk), (q2, q2k))):
                A("act", "activation", [("ps", 2 * s + n)], [("junk", 0), qqk], out=junk[:, 0, n * 512:(n + 1) * 512], in_=ps[2 * s + n][:], func=AF.Square, accum_out=qq)
            t, tk = smcol()
            A("dve", "tensor_tensor", [qk, q2k], [tk], out=t, in0=q, in1=q2, op=ALU.add)
            r, rk = rstd_from(t, tk, D)
            for n in range(2):
                A("dve", "scalar_tensor_tensor", [("ps", 2 * s + n), rk, ("gpost", gidx)], [("tmpf", n)], out=tmpf[:, n, :], in0=ps[2 * s + n][:], scalar=r,
                  in1=gpost[:, gidx, n * 512:(n + 1) * 512], op0=ALU.mult, op1=ALU.mult)
                A("pool" if n == 0 else "dve", "tensor_tensor", [("tmpf", n), XBH(slot, n)], [XBH(slot, n)], out=xb[:, slot, n * 512:(n + 1) * 512], in0=tmpf[:, n, :],
                  in1=xb[:, slot, n * 512:(n + 1) * 512], op=ALU.add)
            if after is not None:
                after(s)

    def load_x(gi):
        if gi >= NT:
            return
        DMA("act", [], XB(gi % 8), xb[:, gi % 8, :], dr["x"][gi * 128:(gi + 1) * 128, :])

    def store_y(gi):
        fin.append(DMA("act", XB(gi % 8), [("y", gi)], y[gi * 128:(gi + 1) * 128, :], xb[:, gi % 8, :]))

    qT = sb("qT", [67, 8, MT], BF16)
    ksT = sb("ksT", [67, 2, T], BF16)
    kwT = sb("kwT", [67, 2, 1024], BF16)
    vs_aug = sb("vs_aug", [128, NT, 2, 65], BF16)
    vw_aug = sb("vw_aug", [128, 8, 2, 65], BF16)
    kc2T = sb("kc2T", [128, 4, 528], BF16)
    kcmpT = sb("kcmpT", [67, 2, NCC * 128], BF16)
    vcmp = sb("vcmp", [128, NCC, 2, 128], BF16)
    expand = sb("expand", [128, T], BF16)
    causalb = sb("causalb", [128, 512], BF16)
    winb = sb("winb", [128, 512], BF16)
    zeros = sb("zeros", [128, 512], BF16)
    fpat = sb("fpat", [128, 3], F32)
    w2 = sb("w2", [128, 2, 2, 64], BF16)
    pe2 = sb("pe2", [128, 2, 16, 2], BF16)
    b1T = sb("b1T", [128, 2, 2], F32)
    wbd = sb("wbd", [128, 2, 4, 128], BF16)
    lcol = sb("lcol", [128, 12, 4], F32)
    hcarry = sb("hcarry", [128, 4], F32)
    xrb = sb("xrb", [128, 4, 515], F32)
    xgb = sb("xgb", [128, 2, 512], F32)
    lruT = sb("lruT", [128, 4, MT], BF16)
    sqacc = sb("sqacc", [128, MT], F32)
    ones_b = sb("ones_b", [128, 2], BF16)
    sqhl = sb("sqhl", [128, 2, MT], BF16)
    PT = sb("PT", [128, 4, 512], BF16)
    selb = sb("selb", [128, 64], BF16)
    selbT = sb("selbT", [128, 4, 128], BF16)
    gates = sb("gates", [128, 4, 24], F32)
    attn = sb("attn", [128, 512], F32)
    attnb = sb("attnb", [128, 512], BF16)
    attnT = sb("attnT", [128, 4, MT], BF16)
    mbuf = tmpf[:].rearrange("p a b -> p (a b)")
    impb = sb("impb", [128, 2, 64], F32)
    m8 = sb("m8", [128, 16], F32)
    hb = sb("hb", [128, 2, 2, 32], BF16)
    vtmp = sb("vtmp", [32, 2, 64], BF16)
    rl_all = sb("rl_all", [128, 4], F32)
    stg = sb("stg", [128, 2, 256], F32)

    hTf = hT[:].rearrange("p a b -> p (a b)").bitcast(F32)

    def wk(j):
        return hTf[:, j * 512:(j + 1) * 512], [("hT", 2 * j), ("hT", 2 * j + 1)]

    DMA("sp", [], ["expand"], expand[:], dr["c_expand"])
    DMA("sp", [], ["causalb"], causalb[:], dr["c_causalb"])
    DMA("sp", [], ["winb"], winb[:], dr["c_winb"])
    DMA("sp", [], ["fpat"], fpat[:], dr["c_fpat"])
    A("pool", "memset", [], ["zeros"], zeros[:], 0.0)
    A("pool", "memset", [], ["selbT"], selbT[:], 0.0)
    A("pool", "memset", [], ["ones_b"], ones_b[:], 1.0)
    A("pool", "memset", [], ["kcmpT"], kcmpT[:], 0.0)
    A("pool", "memset", [], ["vcmp"], vcmp[:], 0.0)
    A("pool", "memset", [], ["kc2T"], kc2T[:], 0.0)
    A("pool", "memset", [], ["xrb"], xrb[:], 0.0)
    A("pool", "memset", [], ["hcarry"], hcarry[:], 0.0)
    A("pool", "memset", [], ["vs_aug"], vs_aug[:], 1.0)
    A("pool", "memset", [], ["vw_aug"], vw_aug[:], 1.0)
    A("pool", "memset", [], ["wbd"], wbd[:], 0.0)
    for g in range(2):
        DMA("sp", [], [("ksT", "aug")], ksT[64:67, g, :], dr["c_kaug"])
        DMA("sp", ["kcmpT"], ["kcmpT"], kcmpT[64:67, g, 0:NCP], dr["c_kcaug"])
        for cc in range(NCC):
            DMA("sp", ["vcmp"], ["vcmp"], vcmp[:, cc, g, 64:128], dr["c_mcs"][cc * 128:(cc + 1) * 128, :])
    for kv, n in enumerate(["cmp_k_w2", "cmp_v_w2"]):
        DMA("sp", [], [("stg", 0)], stg[:, 0, 0:128].rearrange("p (jc d) -> p jc d", d=64), dr[n][0].rearrange("(jc p) d -> p jc d", p=128))
        A("dve", "tensor_copy", [("stg", 0)], ["w2"], out=w2[:, kv, :, :], in_=stg[:, 0, 0:128].rearrange("p (jc d) -> p jc d", d=64))
    pk2 = sb("pk2", [32, 128], F32)
    for kv, n in enumerate(["cmp_k_pe", "cmp_v_pe"]):
        DMA("sp", [], [("pk2", kv)], pk2[16 * kv:16 * kv + 16, :], dr[n][0].rearrange("(lp two) d -> lp (two d)", two=2))
    A("pe", "transpose", [("pk2", 0), ("pk2", 1), "identf"], [("ps", 1)], out=ps[1][:, 0:32], in_=pk2[:], identity=identf[0:32, 0:32])
    for kv in range(2):
        for dup in range(2):
            A("dve", "tensor_copy", [("ps", 1)], ["pe2"], out=pe2[:, kv, :, dup], in_=ps[1][:, 16 * kv:16 * kv + 16])
    for ax, n in enumerate(["lru_w_a", "lru_w_x"]):
        for ch in range(4):
            for hf in range(2):
                DMA("sp", [], [("stg", 0)], stg[hf * 64:(hf + 1) * 64, 0, ch * 64:(ch + 1) * 64], dr[n][0, 2 * ch + hf])
        for ch in range(4):
            for hf in range(2):
                A("dve", "tensor_copy", [("stg", 0)], ["wbd"], out=wbd[hf * 64:(hf + 1) * 64, ax, ch, hf * 64:(hf + 1) * 64],
                  in_=stg[hf * 64:(hf + 1) * 64, 0, ch * 64:(ch + 1) * 64])
    A("dve", "tensor_copy", ["lcol0"], ["lcol"], out=lcol[:, 0:4, :], in_=colT[:, 32:48].rearrange("p (j c) -> p j c", c=4))
    A("dve", "tensor_copy", ["lcol0"], ["lcol"], out=lcol[:, 4:7, :], in_=colT[:, 48:60].rearrange("p (j c) -> p j c", c=4))
    A("dve", "tensor_copy", ["lcol0"], ["lcol"], out=lcol[:, 9, :], in_=colT[:, 60:64])
    A("act", "activation", ["lcol"], ["lcol"], out=lcol[:, 10, :], in_=lcol[:, 9, :], func=AF.Exp, scale=-1.0)
    A("act", "activation", ["lcol", "epsb"], ["lcol"], out=lcol[:, 11, :], in_=lcol[:, 10, :], func=AF.Ln, bias=epsb[:, 1:2])
    A("dve", "tensor_scalar", ["lcol"], ["lcol"], out=lcol[:, 7, :], in0=lcol[:, 11, :], scalar1=-8.0, scalar2=None, op0=ALU.mult)
    A("dve", "tensor_scalar", ["lcol"], ["lcol"], out=lcol[:, 8, :], in0=lcol[:, 11, :], scalar1=-16.0, scalar2=None, op0=ALU.mult)

    C_GELU = 1.5957691216057308

    def gelu_chain(out, x, xk, t1, t1k, n_keys_out, E=None):
        E = E or A
        E("dve", "tensor_tensor", xk, t1k, out=t1, in0=x, in1=x, op=ALU.mult)
        E("dve", "tensor_scalar", t1k, t1k, out=t1, in0=t1, scalar1=0.044715, scalar2=1.0, op0=ALU.mult, op1=ALU.add)
        E("dve", "tensor_tensor", xk + t1k, t1k, out=t1, in0=t1, in1=x, op=ALU.mult)
        E("act", "activation", t1k, t1k, out=t1, in_=t1, func=AF.Sigmoid, scale=C_GELU)
        E("dve", "tensor_tensor", xk + t1k, n_keys_out, out=out, in0=x, in1=t1, op=ALU.mult)

    def evac(eng, out, in_, reads, writes, scale=None):
        if eng == "act":
            if scale is None:
                A("act", "activation", reads, writes, out=out, in_=in_, func=AF.Copy)
            else:
                A("act", "activation", reads, writes, out=out, in_=in_, func=AF.Copy, scale=scale)
        else:
            if scale is None:
                A(eng, "tensor_copy", reads, writes, out=out, in_=in_)
            else:
                A(eng, "tensor_scalar", reads, writes, out=out, in0=in_, scalar1=scale, scalar2=None, op0=ALU.mult)

    ev_rr = [0]

    def ev_eng():
        ev_rr[0] += 1
        return "act" if ev_rr[0] % 2 else "dve"

    def lru_conv(ch):
        xc, xck = wk(ch)
        xcb, xcbk = hT[:, 16 + ch, :], [("hT", 16 + ch)]
        XR = [("xrb", ch)]
        A("dve", "tensor_scalar", XR + ["lcol"], xck, out=xc, in0=xrb[:, ch, 0:512], scalar1=lcol[:, 0, ch:ch + 1], scalar2=lcol[:, 4, ch:ch + 1],
          op0=ALU.mult, op1=ALU.add)
        for j in range(1, 4):
            A("dve", "scalar_tensor_tensor", XR + ["lcol"] + xck, xck, out=xc, in0=xrb[:, ch, j:j + 512], scalar=lcol[:, j, ch:ch + 1], in1=xc,
              op0=ALU.mult, op1=ALU.add)
        A("pool", "tensor_copy", xck, xcbk, out=xcb, in_=xc)
        A("pool", "tensor_copy", XR, XR, out=xrb[:, ch, 0:3], in_=xrb[:, ch, 512:515])

    lru_q = []

    def LQ(*args, **kw):
        lru_q.append(lambda: A(*args, **kw))

    def lru_tick(n=1):
        for _ in range(n):
            if lru_q:
                lru_q.pop(0)()

    def lru_chunk(m, ch, xg_ap, xgk):
        xc, xck = wk(ch)
        xcb, xcbk = hT[:, 16 + ch, :], [("hT", 16 + ch)]
        rr, rrk = wk(4)
        ii, iik = wk(5)
        aa, aak = wk(6)
        mu, muk = wk(7)
        def gate_step():
            A("pe", "matmul", xcbk + ["wbd"], [("ps", 0)], ps[0][:], wbd[:, 0, ch, :], xcb, start=True, stop=True)
            A("pe", "matmul", xcbk + ["wbd"], [("ps", 1)], ps[1][:], wbd[:, 1, ch, :], xcb, start=True, stop=True)
            A("act", "activation", [("ps", 0), "lcol"], rrk, out=rr, in_=ps[0][:], func=AF.Sigmoid, bias=lcol[:, 5, ch:ch + 1])
            A("act", "activation", [("ps", 1), "lcol"], iik, out=ii, in_=ps[1][:], func=AF.Sigmoid, bias=lcol[:, 6, ch:ch + 1])
        lru_q.append(gate_step)
        LQ("act", "activation", rrk + ["lcol"], aak, out=aa, in_=rr, func=AF.Exp, scale=lcol[:, 7, ch:ch + 1])
        LQ("act", "activation", rrk + ["lcol"], muk, out=mu, in_=rr, func=AF.Exp, scale=lcol[:, 8, ch:ch + 1])
        LQ("act", "activation", muk + ["epsb"], muk, out=mu, in_=mu, func=AF.Sqrt, scale=-1.0, bias=epsb[:, 1:2])
        LQ("dve", "tensor_tensor", muk + iik, iik, out=ii, in0=mu, in1=ii, op=ALU.mult)
        LQ("dve", "tensor_tensor", iik + xck, iik, out=ii, in0=ii, in1=xc, op=ALU.mult)
        LQ("dve", "tensor_tensor_scan", aak + iik + ["hcarry"], rrk, out=rr, data0=aa, data1=ii, initial=hcarry[:, ch:ch + 1], op0=ALU.mult, op1=ALU.add)
        LQ("act", "activation", rrk, ["hcarry"], out=hcarry[:, ch:ch + 1], in_=rr[:, 511:512], func=AF.Copy)
        gelu_chain(aa, xg_ap, xgk, mu, muk, aak, E=LQ)
        LQ("dve", "tensor_tensor", rrk + aak, aak, out=aa, in0=rr, in1=aa, op=ALU.mult)
        LQ("act", "activation", aak, [("lruT", ch)], out=lruT[:, ch, :], in_=aa, func=AF.Copy)
        if ch == 0:
            LQ("act", "activation", aak, ["sqacc"], out=sqacc[:], in_=aa, func=AF.Square)
        else:
            LQ("act", "activation", aak, muk, out=mu, in_=aa, func=AF.Square)
            LQ("pool", "tensor_tensor", muk + ["sqacc"], ["sqacc"], out=sqacc[:], in0=mu, in1=sqacc[:], op=ALU.add)

    def in_proj(m):
        T0 = m * MT
        DMA("sp", [], [("qT", "aug")], qT[64:67, :, :], dr["c_qaug"][m].rearrange("r (h t) -> r h t", h=8))
        for g in range(2):
            DMA("sp", [("kwT", "aug")], [("kwT", "aug")], kwT[64:67, g, (T0 % 1024):(T0 % 1024) + MT], dr["c_kaug"][:, T0:T0 + MT])
        import os
        BIS = int(os.environ.get("KBIS", "99"))
        bank = [0]

        def skiprest():
            while wpos[1] < len(wseq) and wseq[wpos[1]].startswith("IN_"):
                w_next("IN_")
                w_issue()

        def nb():
            bank[0] = (bank[0] + 1) % 4
            return bank[0]

        for u in range(4):
            w, wkey = w_next("IN_%d" % u)
            wv = w.rearrange("p (k c) -> p k c", c=128)
            for hh in range(2):
                h = 2 * u + hh
                b = nb()
                for k in range(8):
                    A("pe", "matmul", [wkey] + XNT_ALL, [("ps", b)], ps[b][0:64, :], wv[:, k, hh * 64:(hh + 1) * 64], xnT[:, k, :], start=(k == 0), stop=(k == 7))
                evac(ev_eng(), qT[0:64, h, :], ps[b][0:64, :], [("ps", b)], [("qT", h)], scale=0.125)
            w_issue()
        for kvg in range(4):
            w, wkey = w_next("IN_%d" % (4 + kvg))
            wv = w.rearrange("p (k c) -> p k c", c=128)
            b = nb()
            for k in range(8):
                A("pe", "matmul", [wkey] + XNT_ALL, [("ps", b)], ps[b][:], wv[:, k, :], xnT[:, k, :], start=(k == 0), stop=(k == 7))
            w_issue()
            evac("act", kc2T[0:64, kvg, 16:528], ps[b][0:64, :], [("ps", b)], [("kc2T", kvg)])
            evac("dve", kc2T[64:128, kvg, 15:527], ps[b][64:128, :], [("ps", b)], [("kc2T", kvg)])
        for which in range(2):
            w, wkey = w_next("IN_%d" % (8 + which))
            wv = w.rearrange("p (k c) -> p k c", c=128)
            for g in range(2):
                b = nb()
                for k in range(8):
                    A("pe", "matmul", [wkey] + XNT_ALL, [("ps", b)], ps[b][0:64, :], wv[:, k, g * 64:(g + 1) * 64], xnT[:, k, :], start=(k == 0), stop=(k == 7))
                if which == 0:
                    evac(ev_eng(), ksT[0:64, g, T0:T0 + MT], ps[b][0:64, :], [("ps", b)], [("ksT", g, m)])
                else:
                    c0 = T0 % 1024
                    evac(ev_eng(), kwT[0:64, g, c0:c0 + MT], ps[b][0:64, :], [("ps", b)], [("kwT", g, m % 2)])
            w_issue()
        for ch in range(4):
            w, wkey = w_next("IN_%d" % (10 + ch))
            wv = w.rearrange("p (k c) -> p k c", c=128)
            b = nb()
            for k in range(8):
                A("pe", "matmul", [wkey] + XNT_ALL, [("ps", b)], ps[b][:], wv[:, k, :], xnT[:, k, :], start=(k == 0), stop=(k == 7))
            w_issue()
            evac(ev_eng(), xrb[:, ch, 3:515], ps[b][:], [("ps", b)], [("xrb", ch)])
        for ch in range(4):
            lru_conv(ch)
        def tok_unit(u3):
            w, wkey = w_next("IN_%d" % (18 + u3))
            for kk in range(3):
                k = 3 * u3 + kk
                if k >= 8:
                    continue
                for s in range(4):
                    A("pe", "matmul", [wkey, ("xnT", s)], [("ps", 4 + s)], ps[4 + s][:, 0:280], xnT[:, k, s * 128:(s + 1) * 128], w[:, kk * 280:(kk + 1) * 280],
                      start=(k == 0), stop=(k == 7))
            w_issue()

        for u3 in range(3):
            tok_unit(u3)
        for s in range(4):
            i = 4 * m + s
            b = 4 + s
            evac("dve", vs_aug[:, i, :, 0:64], ps[b][:, 0:128].rearrange("p (g d) -> p g d", d=64), [("ps", b)], [("vs", i)])
            evac("act", vw_aug[:, i % 8, :, 0:64], ps[b][:, 128:256].rearrange("p (g d) -> p g d", d=64), [("ps", b)], [("vw", i % 8)])
            A("act", "activation", [("ps", b)], [("gates", s)], out=gates[:, s, :], in_=ps[b][:, 256:280], func=AF.Sigmoid)

    def lru_step(m, ch):
        def xg_mm():
            w, wkey = w_next("IN_%d" % (14 + ch))
            wv = w.rearrange("p (k c) -> p k c", c=128)
            b = 2 + (ch % 2)
            for k in range(8):
                A("pe", "matmul", [wkey] + XNT_ALL, [("ps", b)], ps[b][:], wv[:, k, :], xnT[:, k, :], start=(k == 0), stop=(k == 7))
            w_issue()
            evac("act", xgb[:, ch % 2, :], ps[b][:], [("ps", b)], [("xgb", ch % 2)])
        lru_q.append(xg_mm)
        lru_chunk(m, ch, xgb[:, ch % 2, :], [("xgb", ch % 2)])

    def lru_stats():
        A("pool", "tensor_copy", ["sqacc"], ["sqhi"], out=sqhl[:, 0, :], in_=sqacc[:])
        A("dve", "tensor_tensor", ["sqacc", "sqhi"], ["sqlo"], out=sqhl[:, 1, :], in0=sqacc[:], in1=sqhl[:, 0, :], op=ALU.subtract)
        for s in range(4):
            A("pe", "matmul", ["sqhi", "ones_b"], [("ps", 0)], ps[0][:, 2 * s:2 * s + 2], sqhl[:, 0, s * 128:(s + 1) * 128], ones_b[:], start=True, stop=False)
            A("pe", "matmul", ["sqlo", "ones_b"], [("ps", 0)], ps[0][:, 2 * s:2 * s + 2], sqhl[:, 1, s * 128:(s + 1) * 128], ones_b[:], start=False, stop=True)
        for s in range(4):
            t, tk = smcol()
            A("act", "activation", [("ps", 0), "epsb"], [tk], out=t, in_=ps[0][:, 2 * s:2 * s + 1], func=AF.Sqrt, scale=1.0 / 512, bias=epsb[:, 0:1])
            A("dve", "reciprocal", [tk], [("rl", s)], out=rl_all[:, s:s + 1], in_=t)

    def compress(m):
        for kv in range(2):
            for u in range(4):
                w, wkey = w_next(("C1K_%d" if kv == 0 else "C1V_%d") % u)
                wv = w.rearrange("p (l j) -> p l j", j=256)
                for l4 in range(4):
                    lp = 4 * u + l4
                    for g in range(2):
                        for jc in range(2):
                            b = g * 2 + jc
                            A("pe", "matmul", [wkey, ("kc2T", 2 * kv + g)], [("ps", b)], ps[b][:, 0:32], wv[:, l4, jc * 128:(jc + 1) * 128],
                              kc2T[:, 2 * kv + g, 2 * lp:2 * lp + 497:16], start=(lp == 0), stop=(lp == 15))
                    if m == 0:
                        for jc in range(2):
                            A("pe", "matmul", [wkey, "pe2"], [("ps", 4 + jc)], ps[4 + jc][:, 0:2], wv[:, l4, jc * 128:(jc + 1) * 128], pe2[:, kv, lp, :],
                              start=(lp == 0), stop=(lp == 15))
                w_issue()
            if m == 0:
                for jc in range(2):
                    A("dve", "tensor_copy", [("ps", 4 + jc)], ["b1T"], out=b1T[:, kv, jc:jc + 1], in_=ps[4 + jc][:, 0:1])
            for g in range(2):
                for jc in range(2):
                    b = g * 2 + jc
                    xh = stg[:, 0, (g * 2 + jc) * 32:(g * 2 + jc + 1) * 32]
                    th = stg[:, 1, (g * 2 + jc) * 32:(g * 2 + jc + 1) * 32]
                    xk = [("stgx", g, jc)]
                    tk = [("stgt", g, jc)]
                    A("act", "activation", [("ps", b), "b1T", ("stg", 0)], xk, out=xh, in_=ps[b][:, 0:32], func=AF.Identity, bias=b1T[:, kv, jc:jc + 1])
                    gelu_chain(hb[:, g, jc, :], xh, xk, th, tk + [("stg", 1)], [("hb", g, jc)])
            q4 = m % 4
            cc = m // 4
            for g in range(2):
                if kv == 0:
                    for jc in range(2):
                        A("pe", "matmul", [("hb", g, jc), "w2"], [("ps", 4 + g)], ps[4 + g][0:64, 0:32], w2[:, 0, jc, :], hb[:, g, jc, :], start=(jc == 0), stop=(jc == 1))
                    evac(ev_eng(), kcmpT[0:64, g, 32 * m:32 * m + 32], ps[4 + g][0:64, 0:32], [("ps", 4 + g)], ["kcmpT"])
                else:
                    for jc in range(2):
                        A("pe", "matmul", [("hb", g, jc), "w2"], [("ps", 6 + g)], ps[6 + g][0:32, 0:64], hb[:, g, jc, :], w2[:, 1, jc, :],
                          start=(jc == 0), stop=(jc == 1))
                    evac(ev_eng(), vtmp[:, g, :], ps[6 + g][0:32, 0:64], [("ps", 6 + g)], [("vtmp", g)])
                    if m == 0:
                        A("dve", "memset", [], [("vtmp", g)], vtmp[0:1, g, :], 0.0)
                    DMA("act", [("vtmp", g)], ["vcmp"], vcmp[32 * q4:32 * q4 + 32, cc, g, 0:64], vtmp[:, g, :])
        for kvg in range(4):
            A("pool", "tensor_copy", [("kc2T", kvg)], [("kc2T", kvg)], out=kc2T[:, kvg, 0:16], in_=kc2T[:, kvg, 512:528])
    BIGF = 1e9
    TINY = 1e-30

    def attention(m, s):
        i = 4 * m + s
        for g in range(2):
            qg = qT[0:67, 4 * g:4 * g + 4, s * 128:(s + 1) * 128]
            QK = [("qT", h) for h in range(4 * g, 4 * g + 4)] + [("qT", "aug")]
            ncv = 8 * i + 8
            ncc = (ncv + 127) // 128
            for cc in range(ncc):
                Mc = min(128, ncv - 128 * cc)
                b = cc
                A("pe", "matmul", QK + ["kcmpT"], [("ps", b)], ps[b][0:Mc, :], kcmpT[0:67, g, cc * 128:cc * 128 + Mc], qg, start=True, stop=True)
                A("act", "activation", [("ps", b)], [("PT", cc)], out=PT[0:Mc, cc, :], in_=ps[b][0:Mc, :], func=AF.Exp)
                A("pool", "affine_select", [("PT", cc)], [("PT", cc)], out=PT[0:Mc, cc, :], in_=PT[0:Mc, cc, :], pattern=[[0, 4], [1, 128]],
                  compare_op=ALU.is_ge, fill=0.0, base=128 * i - 2048 * cc - 15, channel_multiplier=-16)
            for h in range(4):
                for cc in range(ncc):
                    Mc = min(128, ncv - 128 * cc)
                    A("pe", "matmul", [("PT", cc), "vcmp"], [("ps", 4)], ps[4][:, h * 128:(h + 1) * 128], PT[0:Mc, cc, h * 128:(h + 1) * 128], vcmp[0:Mc, cc, g, :],
                      start=(cc == 0), stop=(cc == ncc - 1))
            pc = ps[4][:].rearrange("p (h c) -> p h c", c=128)
            l4, l4k = small[:, 40:44], ("sm", "l4")
            r4, r4k = small[:, 44:48], ("sm", "r4")
            A("dve", "tensor_reduce", [("ps", 4)], [l4k], out=l4, in_=pc[:, :, 64:128], axis=AX.X, op=ALU.add)
            A("dve", "tensor_scalar", [l4k], [l4k], out=l4, in0=l4, scalar1=TINY, scalar2=None, op0=ALU.max)
            A("dve", "reciprocal", [l4k], [r4k], out=r4, in_=l4)
            imp = impb[:, 0, :]
            imp2 = impb[:, 1, :]
            A("dve", "tensor_scalar", [("ps", 4), r4k], ["imp"], out=imp, in0=pc[:, 0, 64:128], scalar1=r4[:, 0:1], scalar2=None, op0=ALU.mult)
            for h in range(1, 4):
                A("dve", "scalar_tensor_tensor", [("ps", 4), r4k, "imp"], ["imp"], out=imp, in0=pc[:, h, 64:128], scalar=r4[:, h:h + 1], in1=imp, op0=ALU.mult, op1=ALU.add)
            A("dve", "memset", [], ["imp"], impb[:, 0, 0:1], BIGF)
            lo, hi = max(0, 2 * i - 1), min(64, 2 * i + 2)
            A("dve", "tensor_tensor", ["imp", "fpat"], ["imp"], out=impb[:, 0, lo:hi], in0=impb[:, 0, lo:hi], in1=fpat[:, lo - (2 * i - 1):hi - (2 * i - 1)], op=ALU.max)
            A("dve", "max", ["imp"], ["m8a"], out=m8[:, 0:8], in_=imp)
            A("dve", "match_replace", ["imp", "m8a"], ["imp2"], out=imp2, in_to_replace=m8[:, 0:8], in_values=imp, imm_value=-BIGF)
            A("dve", "max", ["imp2"], ["m8b"], out=m8[:, 8:16], in_=imp2)
            A("dve", "tensor_scalar", ["imp", "m8b"], ["selb"], out=selb[:], in0=imp, scalar1=m8[:, 15:16], scalar2=-BIG, op0=ALU.is_lt, op1=ALU.mult)
            gv = gates[:, s, :].rearrange("p (h b) -> p h b", b=3)
            cf, cfk = small[:, 48:52], ("sm", "cf")
            A("dve", "tensor_tensor", [r4k, ("gates", s)], [cfk], out=cf, in0=r4, in1=gv[:, 4 * g:4 * g + 4, 0], op=ALU.mult)
            for h in range(4):
                hh = 4 * g + h
                A("dve", "tensor_scalar", [("ps", 4), cfk], [("attn", hh)], out=attn[:, hh * 64:(hh + 1) * 64], in0=pc[:, h, 0:64], scalar1=cf[:, h:h + 1], scalar2=None, op0=ALU.mult)
            for br in (1, 0):
                ob = 5 + br
                if br == 0:
                    pT7 = ps[7][:].bitcast(BF16)
                    A("pe", "transpose", ["selb", "ident"], [("ps", 7)], out=pT7[0:64, 0:128], in_=selb[:], identity=ident[:])
                    for h in range(4):
                        evac("act" if h % 2 else "dve", selbT[0:64, h, :], pT7[0:64, 0:128], [("ps", 7)], ["selbT"])
                A("pe", "matmul", ["zeros"], [("ps", ob)], ps[ob][:, 0:260], zeros[:, 0:128], zeros[:, 0:260], start=True, stop=True)
                kcs = list(range(0, i + 1)) if br == 0 else list(range(max(0, i - 4), i + 1))

                def emit_S(n_, kc, br=br, ob=ob):
                    b = n_ % 4
                    j = n_ % 4
                    if br == 0:
                        A("pe", "matmul", QK + [("ksT", g, kc // 4), ("ksT", "aug")], [("ps", b)], ps[b][:], ksT[0:67, g, kc * 128:(kc + 1) * 128], qg, start=True, stop=False)
                        A("pe", "matmul", ["expand", "selbT"], [("ps", b)], ps[b][:], expand[:, kc * 128:(kc + 1) * 128], selbT[:].rearrange("p h q -> p (h q)"),
                          start=False, stop=(kc != i))
                    else:
                        c0 = (kc * 128) % 1024
                        last = not (kc == i or kc == i - 4)
                        A("pe", "matmul", QK + [("kwT", g, (kc // 4) % 2), ("kwT", "aug")], [("ps", b)], ps[b][:], kwT[0:67, g, c0:c0 + 128], qg, start=True, stop=last)
                        if kc == i - 4:
                            A("pe", "matmul", ["ident", "winb"], [("ps", b)], ps[b][:], ident[:], winb[:], start=False, stop=True)
                    if kc == i:
                        A("pe", "matmul", ["ident", "causalb"], [("ps", b)], ps[b][:], ident[:], causalb[:], start=False, stop=True)
                    A("act", "activation", [("ps", b)], [("PT", j)], out=PT[:, j, :], in_=ps[b][:], func=AF.Exp)

                def emit_PV(n_, kc, br=br, ob=ob):
                    j = n_ % 4
                    for h in range(4):
                        if br == 0:
                            A("pe", "matmul", [("PT", j), ("vs", kc)], [("ps", ob)], ps[ob][:, h * 65:(h + 1) * 65], PT[:, j, h * 128:(h + 1) * 128], vs_aug[:, kc, g, :],
                              start=False, stop=(kc == i), skip_group_check=True)
                        else:
                            A("pe", "matmul", [("PT", j), ("vw", kc % 8)], [("ps", ob)], ps[ob][:, h * 65:(h + 1) * 65], PT[:, j, h * 128:(h + 1) * 128], vw_aug[:, kc % 8, g, :],
                              start=False, stop=(kc == i), skip_group_check=True)

                LA = 2
                for n_ in range(len(kcs) + LA):
                    if n_ < len(kcs):
                        emit_S(n_, kcs[n_])
                    if n_ - LA >= 0:
                        emit_PV(n_ - LA, kcs[n_ - LA])
                    lru_tick(2)
                po = ps[ob][:, 0:260].rearrange("p (h c) -> p h c", c=65)
                lb, lbk = small[:, 52 + 8 * br:56 + 8 * br], ("sm", "lb", br)
                cb, cbk = small[:, 56 + 8 * br:60 + 8 * br], ("sm", "cb", br)
                A("dve", "tensor_scalar", [("ps", ob)], [lbk], out=lb, in0=po[:, :, 64], scalar1=TINY, scalar2=None, op0=ALU.max)
                A("dve", "reciprocal", [lbk], [lbk], out=lb, in_=lb)
                A("dve", "tensor_tensor", [lbk, ("gates", s)], [cbk], out=cb, in0=lb, in1=gv[:, 4 * g:4 * g + 4, 1 + br], op=ALU.mult)
                for h in range(4):
                    hh = 4 * g + h
                    A("dve", "scalar_tensor_tensor", [("ps", ob), cbk, ("attn", hh)], [("attn", hh)], out=attn[:, hh * 64:(hh + 1) * 64], in0=po[:, h, 0:64],
                      scalar=cb[:, h:h + 1], in1=attn[:, hh * 64:(hh + 1) * 64], op0=ALU.mult, op1=ALU.add)
        AK = [("attn", hh) for hh in range(8)]
        q, qk = smcol()
        A("act", "activation", AK, [("junk", 0), qk], out=junk[:, 0, 0:512], in_=attn[:], func=AF.Square, accum_out=q)
        ra, rak = rstd_from(q, qk, 512)
        A("pool", "tensor_copy", AK, ["attnb"], out=attnb[:], in_=attn[:])
        pT7 = ps[7][:].bitcast(BF16)
        for c in range(4):
            A("pe", "transpose", ["attnb", "ident"], [("ps", 7)], out=pT7[:, c * 128:(c + 1) * 128], in_=attnb[:, c * 128:(c + 1) * 128], identity=ident[:])
        evac("dve", attnT[:, :, s * 128:(s + 1) * 128], pT7[:, 0:512].rearrange("p (c q) -> p c q", q=128), [("ps", 7)], [("attnT", s)])
        return ra, rak

    def out_proj_a(m, s, ra, rak):
        gi = 4 * m + s
        slot = gi % 8
        for u in range(8):
            w, wkey = w_next("WO_%d" % u)
            n, kp = divmod(u, 4)
            isl = kp >= 2
            b = 2 * n + (1 if isl else 0)
            for kq in range(2):
                k = 2 * kp + kq
                if isl:
                    lhs, lk = lruT[:, k - 4, s * 128:(s + 1) * 128], [("lruT", k - 4)]
                else:
                    lhs, lk = attnT[:, k, s * 128:(s + 1) * 128], [("attnT", s)]
                A("pe", "matmul", [wkey] + lk, [("ps", b)], ps[b][:], lhs, w[:, kq * 512:(kq + 1) * 512], start=(k % 4 == 0), stop=(k % 4 == 3))
            w_issue()
        for n in range(2):
            A("dve", "tensor_scalar", [("ps", 2 * n), rak], [("tmpf", n)], out=mbuf[:, n * 512:(n + 1) * 512], in0=ps[2 * n][:], scalar1=ra, scalar2=None, op0=ALU.mult)
            A("dve", "scalar_tensor_tensor", [("ps", 2 * n + 1), ("rl", s), ("tmpf", n)], [("tmpf", n)], out=mbuf[:, n * 512:(n + 1) * 512], in0=ps[2 * n + 1][:],
              scalar=rl_all[:, s:s + 1], in1=mbuf[:, n * 512:(n + 1) * 512], op0=ALU.mult, op1=ALU.add)

    def out_proj_b(m, s):
        gi = 4 * m + s
        slot = gi % 8
        q, qk = smcol()
        A("act", "activation", [("tmpf", 0), ("tmpf", 1)], [("junk", 0), qk], out=junk[:, 0, :], in_=mbuf, func=AF.Square, accum_out=q)
        r, rk = rstd_from(q, qk, D)
        A("dve", "scalar_tensor_tensor", [("tmpf", 0), ("tmpf", 1), rk, ("gpost", 1)], [("tmpf", 0), ("tmpf", 1)], out=mbuf, in0=mbuf, scalar=r, in1=gpost[:, 1, :],
          op0=ALU.mult, op1=ALU.mult)
        A("pool", "tensor_tensor", [("tmpf", 0), ("tmpf", 1)] + XB(slot), XB(slot), out=xb[:, slot, :], in0=mbuf, in1=xb[:, slot, :], op=ALU.add)

    for gi in range(8):
        load_x(gi)
    for s in range(4):
        prenorm_T(s, s)
    for m in range(NM):
        ffn(m, "1", 0)
        for s in range(4):
            prenorm_T(4 * m + s, s)
        in_proj(m)
        compress(m)
        for ch in range(4):
            lru_step(m, ch)
        ras = [attention(m, 0), attention(m, 1)]
        lru_tick(100000)
        lru_stats()
        out_proj_a(m, 0, *ras[0])
        for s in (2, 3):
            ras.append(attention(m, s))
            out_proj_b(m, s - 2)
            out_proj_a(m, s - 1, *ras[s - 1])
            prenorm_a(4 * m + s - 2)
            if s > 2:
                prenorm_b(4 * m + s - 3, s - 3)
        out_proj_b(m, 2)
        out_proj_a(m, 3, *ras[3])
        prenorm_a(4 * m + 2)
        prenorm_b(4 * m + 1, 1)
        out_proj_b(m, 3)
        prenorm_a(4 * m + 3)
        prenorm_b(4 * m + 2, 2)
        prenorm_b(4 * m + 3, 3)

        def after2(s, m=m):
            store_y(4 * m + s)
            load_x(4 * (m + 2) + s)

        def first2(m=m):
            if m + 1 < NM:
                prenorm_a(4 * (m + 1))
                prenorm_a(4 * (m + 1) + 1)

        def mid2(m=m):
            if m + 1 < NM:
                g0 = 4 * (m + 1)
                prenorm_b(g0, 0)
                prenorm_b(g0 + 1, 1)
                prenorm_a(g0 + 2)
                prenorm_a(g0 + 3)
                prenorm_b(g0 + 2, 2)
                prenorm_b(g0 + 3, 3)

        ffn(m, "2", 2, after=after2, mid=mid2, first=first2)

    if recording:
        return rec
    with nc.Block() as block:
        semstack = S.emit(nc, block, None, fin)
    return nc, consts


def build_program(T):
    wseq = build(T, stage=9, wseq=None)
    return build(T, stage=9, wseq=wseq)


def kernel(**inputs):
    x = np.ascontiguousarray(np.asarray(inputs["x"], dtype=np.float32))
    B, T, _ = x.shape
    nc, consts = build_program(T)
    in_maps = []
    for b in range(B):
        im = {"x": x[b]}
        for n in WNAMES:
            im[n] = np.ascontiguousarray(np.asarray(inputs[n], dtype=np.float32))
        im.update(consts)
        in_maps.append(im)
    res = run_bass_kernel_spmd(nc, in_maps, core_ids=list(range(B)))
    out = np.stack([np.asarray(r["y"], dtype=np.float32) for r in res.results], axis=0)
    return out
```

```python
import contextlib
import numpy as np
import concourse.bass as bass
import concourse.mybir as mybir
from concourse.bass_utils import run_bass_kernel_spmd

F32 = mybir.dt.float32
BF16 = mybir.dt.bfloat16
AF = mybir.ActivationFunctionType
ALU = mybir.AluOpType
AX = mybir.AxisListType

D = 1024
DFF = 2816
NF = DFF // 128
MT = 512
NQH = 8
DH = 64
NG = 2
INW = 2328
EPS = 1e-6
BIG = 30000.0
O_Q, O_KC, O_VC, O_KS, O_VS, O_KW, O_VW, O_GL, O_XR, O_XG = 0, 512, 640, 768, 896, 1024, 1152, 1280, 1304, 1816


class Op:
    __slots__ = ("eng", "fn", "deps", "dma", "sig", "tok", "idx")

    def __init__(self, eng, fn, deps, dma):
        self.eng = eng
        self.fn = fn
        self.deps = deps
        self.dma = dma
        self.sig = False
        self.tok = None


class Sched:
    ENGS = ("pe", "act", "dve", "pool", "sp")
    EPOCH = 24000
    DMA_POOL = {"sp": 8, "pool": 4, "act": 4}

    def __init__(self):
        self.q = {e: [] for e in self.ENGS}
        self.lastw = {}
        self.readers = {}
        self.all_ops = []
        self.dma_hist = {e: [] for e in self.DMA_POOL}

    def add(self, eng, fn, reads=(), writes=(), dma=False):
        deps = []
        for k in reads:
            w = self.lastw.get(k)
            if w is not None:
                deps.append(w)
            if isinstance(k, tuple) and k[0] == "ps":
                for r in self.readers.get(k, ()):
                    if r.eng != eng:
                        deps.append(r)
        for k in writes:
            w = self.lastw.get(k)
            if w is not None:
                deps.append(w)
            deps.extend(self.readers.get(k, ()))
        op = Op(eng, fn, None, dma)
        if dma:
            hist = self.dma_hist[eng]
            P = self.DMA_POOL[eng]
            if len(hist) >= P:
                deps.append(hist[len(hist) - P])
            hist.append(op)
        dd = []
        seen = set()
        for d in deps:
            if id(d) in seen:
                continue
            seen.add(id(d))
            if (not dma) and (not d.dma) and d.eng == "pe" and eng == "pe":
                continue
            dd.append(d)
            d.sig = True
        op.deps = dd
        for k in reads:
            self.readers.setdefault(k, []).append(op)
        for k in writes:
            self.lastw[k] = op
            self.readers[k] = []
        self.q[eng].append(op)
        self.all_ops.append(op)
        return op

    def emit(self, nc, block, engines, final_waits):
        n_epochs = {}
        for e in self.ENGS:
            n = sum(1 for o in self.q[e] if (o.sig and not o.dma))
            n_epochs[e] = max(1, (n + self.EPOCH - 1) // self.EPOCH)
        stack = contextlib.ExitStack()
        sems = {}
        for e in self.ENGS:
            sems[e] = [stack.enter_context(nc.semaphore("s_%s_%d" % (e, i))) for i in range(n_epochs[e])]
        dsems = {}
        for e, P in self.DMA_POOL.items():
            if self.dma_hist[e]:
                dsems[e] = [stack.enter_context(nc.semaphore("d_%s_%d" % (e, i))) for i in range(min(P, len(self.dma_hist[e])))]
        for e in self.ENGS:
            c = 0
            for o in self.q[e]:
                if o.dma:
                    continue
                if o.sig:
                    o.tok = (sems[e][c // self.EPOCH], c % self.EPOCH + 1)
                    c += 1
        for e, hist in self.dma_hist.items():
            P = self.DMA_POOL[e]
            for j, o in enumerate(hist):
                o.tok = (dsems[e][j % P], 16 * (j // P + 1))
        fw = [o.tok for o in final_waits]

        def run_engine(ename):
            def body(eng):
                waited = {}
                for o in self.q[ename]:
                    for d in o.deps:
                        s, v = d.tok
                        if waited.get(id(s), 0) >= v:
                            continue
                        eng.wait_ge(s, v)
                        waited[id(s)] = v
                    ins = o.fn(eng)
                    if o.dma:
                        ins.then_inc(o.tok[0], 16)
                    elif o.sig:
                        ins.then_inc(o.tok[0], 1)
                if ename == "sp":
                    for s, v in fw:
                        eng.wait_ge(s, v)
            return body

        block.tensor(run_engine("pe"))
        block.scalar(run_engine("act"))
        block.vector(run_engine("dve"))
        block.gpsimd(run_engine("pool"))
        block.sync(run_engine("sp"))
        return stack


def unit_directory():
    names = []
    for pre in ("1", "2"):
        names += ["G%s_%d" % (pre, f) for f in range(NF)]
        names += ["U%s_%d" % (pre, f) for f in range(NF)]
        names += ["D%s_%d" % (pre, f) for f in range(NF)]
    names += ["IN_%d" % u for u in range(21)]
    names += ["C1K_%d" % u for u in range(4)] + ["C1V_%d" % u for u in range(4)]
    names += ["WO_%d" % u for u in range(8)]
    return {n: i for i, n in enumerate(names)}


UNITS = unit_directory()
NUNITS = len(UNITS)


def macro_units(m_first):
    seq = []
    for pre in ("1",):
        for f in range(NF):
            seq += ["G1_%d" % f, "U1_%d" % f]
        seq += ["D1_%d" % f for f in range(NF)]
    seq += ["IN_%d" % u for u in range(21)]
    seq += ["C1K_%d" % u for u in range(4)] + ["C1V_%d" % u for u in range(4)]
    for s in range(4):
        seq += ["WO_0", "WO_1", "WO_2", "WO_3", "WO_4", "WO_5", "WO_6", "WO_7"]
    for f in range(NF):
        seq += ["G2_%d" % f, "U2_%d" % f]
    seq += ["D2_%d" % f for f in range(NF)]
    return seq


def host_consts(T):
    import ml_dtypes
    bf = ml_dtypes.bfloat16
    pos = np.arange(T)
    kaug = np.stack([pos // 128, pos % 128, np.ones(T)]).astype(np.float32)
    slopes = np.power(2.0, -np.arange(1, 9)).astype(np.float64)
    ntile = T // 128
    qaug = np.zeros((T // MT, 3, 8, MT), np.float32)
    for i in range(ntile):
        m, s = divmod(i, 4)
        qaug[m, 0, :, s * 128:(s + 1) * 128] = (128.0 * slopes)[:, None]
        qaug[m, 1, :, s * 128:(s + 1) * 128] = slopes[:, None]
        qaug[m, 2, :, s * 128:(s + 1) * 128] = (-slopes * 128.0 * i)[:, None]
    ncp = T // 16
    cpos = 16 * np.arange(ncp) + 15
    kcaug = np.stack([cpos // 128, cpos % 128, np.ones(ncp)]).astype(np.float32)
    kcaug[:, 0] = 0.0
    ncpad = ((ncp + 127) // 128) * 128
    mcs = np.zeros((ncpad, 64), np.float32)
    for cp in range(1, ncp):
        c = cp - 1
        for j in range(64):
            ov = min(16 * c + 32, 64 * j + 64) - max(16 * c, 64 * j)
            if ov > 0:
                mcs[cp, j] = ov / 32.0
    expand = np.zeros((128, T), np.float32)
    for k in range(T):
        expand[k // 64, k] = 1.0 if k // 64 < 64 else 0.0
    kk = np.arange(128)[:, None]
    qq = np.arange(128)[None, :]
    causal = np.where(kk > qq, -BIG, 0.0).astype(np.float32)
    winm = np.where(kk <= qq, -BIG, 0.0).astype(np.float32)
    causalb = np.tile(causal, (1, 4))
    winb = np.tile(winm, (1, 4))
    ident = np.eye(128, dtype=np.float32)
    fpat = np.zeros((128, 3), np.float32)
    fpat[:64, 0] = 1e9
    fpat[:64, 1] = 1e9
    fpat[64:, 1] = 1e9
    fpat[64:, 2] = 1e9
    c = {
        "c_kaug": kaug.astype(bf), "c_qaug": qaug.reshape(T // MT, 3, 8 * MT).astype(bf),
        "c_kcaug": kcaug.astype(bf), "c_mcs": mcs.astype(bf), "c_expand": expand.astype(bf),
        "c_causalb": causalb.astype(bf), "c_winb": winb.astype(bf), "c_ident": ident.astype(bf),
        "c_fpat": fpat, "c_identf": np.eye(128, dtype=np.float32),
    }
    return c


WNAMES = ["ffn1_pre_g", "ffn1_post_g", "ffn1_w_gate", "ffn1_w_up", "ffn1_w_down", "mix_pre_g", "mix_post_g",
          "w_in", "cmp_k_pe", "cmp_k_w1", "cmp_k_w2", "cmp_v_pe", "cmp_v_w1", "cmp_v_w2", "conv_w", "conv_b",
          "lru_w_a", "lru_b_a", "lru_w_x", "lru_b_x", "lru_lambda", "attn_out_g", "lru_out_g", "w_out",
          "ffn2_pre_g", "ffn2_post_g", "ffn2_w_gate", "ffn2_w_up", "ffn2_w_down"]
WSHAPES = {
    "ffn1_pre_g": [1, D], "ffn1_post_g": [1, D], "ffn1_w_gate": [1, D, DFF], "ffn1_w_up": [1, D, DFF],
    "ffn1_w_down": [1, DFF, D], "mix_pre_g": [1, D], "mix_post_g": [1, D], "w_in": [1, D, INW],
    "cmp_k_pe": [1, 32, 64], "cmp_k_w1": [1, 2048, 256], "cmp_k_w2": [1, 256, 64],
    "cmp_v_pe": [1, 32, 64], "cmp_v_w1": [1, 2048, 256], "cmp_v_w2": [1, 256, 64],
    "conv_w": [1, 4, 512], "conv_b": [1, 512], "lru_w_a": [1, 8, 64, 64], "lru_b_a": [1, 512],
    "lru_w_x": [1, 8, 64, 64], "lru_b_x": [1, 512], "lru_lambda": [1, 512], "attn_out_g": [1, 512],
    "lru_out_g": [1, 512], "w_out": [1, D, D], "ffn2_pre_g": [1, D], "ffn2_post_g": [1, D],
    "ffn2_w_gate": [1, D, DFF], "ffn2_w_up": [1, D, DFF], "ffn2_w_down": [1, DFF, D],
}


def build(T, stage=9, NS=10, wseq=None):
    NM = T // MT
    NT = T // 128
    NCP = T // 16
    NCC = (NCP + 127) // 128
    consts = host_consts(T)
    nc = bass.Bass("TRN2", target_bir_lowering=False)
    dr = {}
    dr["x"] = nc.dram_tensor("x", [T, D], F32, kind="ExternalInput").ap()
    for n in WNAMES:
        dr[n] = nc.dram_tensor(n, WSHAPES[n], F32, kind="ExternalInput").ap()
    for n, a in consts.items():
        dr[n] = nc.dram_tensor(n, list(a.shape), F32 if a.dtype == np.float32 else BF16, kind="ExternalInput").ap()
    y = nc.dram_tensor("y", [T, D], F32, kind="ExternalOutput").ap()
    wscr = nc.dram_tensor("wscr", [NUNITS, 128, 1024], BF16, kind="Internal").ap()

    S = Sched()
    es = contextlib.ExitStack()

    def sb(name, shape, dt):
        return es.enter_context(nc.sbuf_tensor(name, shape, dt))

    def A(eng, meth, reads, writes, *args, **kw):
        dma = kw.pop("_dma", False)
        return S.add(eng, lambda e: getattr(e, meth)(*args, **kw), reads, writes, dma=dma)

    def DMA(eng, reads, writes, out, in_, slow=False):
        if slow:
            return S.add(eng, lambda e: e.dma_start(out=out, in_=in_, allow_slow_non_contiguous=True), reads, writes, dma=True)
        return S.add(eng, lambda e: e.dma_start(out=out, in_=in_), reads, writes, dma=True)

    xb = sb("xb", [128, 8, 1024], F32)
    xn = sb("xn", [128, 2, 1024], BF16)
    xnT = sb("xnT", [128, 8, MT], BF16)
    hT = sb("hT", [128, NF, MT], BF16)
    ring = sb("ring", [128, NS, 1024], BF16)
    gpost = sb("gpost", [128, 3, 1024], F32)
    ident = sb("ident", [128, 128], BF16)
    small = sb("small", [128, 80], F32)
    junk = sb("junk", [128, 1, 1024], BF16)
    sg = sb("sg", [128, 2, MT], F32)
    tmpf = sb("tmpf", [128, 2, 512], F32)
    ps = [es.enter_context(nc.psum_tensor("ps%d" % b, [128, 512], F32)) for b in range(8)]

    st_f32 = hT[:].rearrange("p a b -> p (a b)").bitcast(F32)
    st_bf = xb[:].rearrange("p a b -> p (a b)").bitcast(BF16)

    fin = []

    DMA("sp", [], ["ident"], ident[:], dr["c_ident"])
    identf = sb("identf", [128, 128], F32)
    DMA("sp", [], ["identf"], identf[:], dr["c_identf"])
    pk = sb("pk", [64, 128], F32)
    colT = sb("colT", [128, 64], F32)
    for j, n in enumerate(["ffn1_pre_g", "mix_pre_g", "ffn2_pre_g"]):
        DMA("sp", [], [("pk", j)], pk[8 * j:8 * j + 8, :], dr[n][0].rearrange("(k p) -> k p", p=128))
    DMA("sp", [], [("pk", 3)], pk[24:28, :], dr["attn_out_g"][0].rearrange("(k p) -> k p", p=128))
    DMA("sp", [], [("pk", 4)], pk[28:32, :], dr["lru_out_g"][0].rearrange("(k p) -> k p", p=128))
    DMA("sp", [], [("pk", 5)], pk[32:48, :], dr["conv_w"][0].rearrange("j (c p) -> (j c) p", p=128))
    for j, n in enumerate(["conv_b", "lru_b_a", "lru_b_x", "lru_lambda"]):
        DMA("sp", [], [("pk", 6 + j)], pk[48 + 4 * j:52 + 4 * j, :], dr[n][0].rearrange("(c p) -> c p", p=128))
    A("pe", "transpose", [("pk", j) for j in range(10)] + ["identf"], [("ps", 0)], out=ps[0][:, 0:64], in_=pk[:], identity=identf[0:64, 0:64])
    A("dve", "tensor_copy", [("ps", 0)], ["gcol", "gocol", "lcol0"], out=colT[:], in_=ps[0][:, 0:64])
    gcol = colT[:, 0:24].rearrange("p (j k) -> p j k", k=8)
    gocol = colT[:, 24:32]
    for j, n in enumerate(["ffn1_post_g", "mix_post_g", "ffn2_post_g"]):
        DMA("sp", [], [("gpost", j)], gpost[:, j, :], dr[n][0:1, :].partition_broadcast(128))
    for j in (0, 2):
        A("pool", "tensor_scalar", [("gpost", j)], [("gpost", j)], out=gpost[:, j, :], in0=gpost[:, j, :], scalar1=0.5,
          scalar2=None, op0=ALU.mult)

    cast_rr = [0]

    def cast(out, in_, scale, reads, writes):
        e = ("dve", "act")[cast_rr[0] % 2]
        cast_rr[0] += 1
        if e == "act":
            if scale is None:
                A("act", "activation", reads, writes, out=out, in_=in_, func=AF.Copy)
            else:
                A("act", "activation", reads, writes, out=out, in_=in_, func=AF.Copy, scale=scale)
        else:
            if scale is None:
                A(e, "tensor_copy", reads, writes, out=out, in_=in_)
            else:
                A(e, "tensor_scalar", reads, writes, out=out, in0=in_, scalar1=scale, scalar2=None, op0=ALU.mult)

    ld_rr = [0]

    def stage_load(src, width):
        j = ld_rr[0] % 2
        ld_rr[0] += 1
        v = st_f32[:, j * 2816: j * 2816 + width]
        DMA("sp", [], [("stf", j)], v, src)
        return v, ("stf", j)

    STB = [("stb", j) for j in range(24)]

    def flush(unit_names, nel):
        for i, un in enumerate(unit_names):
            DMA("act" if i % 2 else "sp", STB, [("wscr", un)], wscr[UNITS[un]], st_bf[:, i * 1024:(i + 1) * 1024])

    def conv_gate_like(wname, gidx, prefix):
        for fh in range(2):
            stv = st_bf[:, 0:11 * 1024].rearrange("p (f k c) -> p f k c", f=11, k=8, c=128)
            for k in range(8):
                v, key = stage_load(dr[wname][0, k * 128:(k + 1) * 128, fh * 1408:(fh + 1) * 1408], 1408)
                cast(stv[:, :, k, :], v.rearrange("p (f c) -> p f c", c=128), gcol[:, gidx, k:k + 1], [key, "gcol"], [("stb", k)])
            flush(["%s_%d" % (prefix, fh * 11 + f) for f in range(11)], 11 * 1024)

    def conv_down(wname, prefix):
        for fh in range(2):
            for f in range(11):
                v, key = stage_load(dr[wname][0, (fh * 11 + f) * 128:(fh * 11 + f + 1) * 128, :], 1024)
                cast(st_bf[:, f * 1024:(f + 1) * 1024], v, None, [key], [("stb", f)])
            flush(["%s_%d" % (prefix, fh * 11 + f) for f in range(11)], 11 * 1024)

    def conv_win():
        stv = st_bf[:, 0:11 * 1024]
        wsA = stv[:, 0:10 * 1024].rearrange("p (u k c) -> p u k c", u=10, k=8, c=128)
        for k in range(8):
            v, key = stage_load(dr["w_in"][0, k * 128:(k + 1) * 128, 0:1152], 1152)
            sc = gcol[:, 1, k:k + 1]
            R, W = [key, "gcol"], [("stb", k)]
            cast(wsA[:, 0:4, k, :], v[:, O_Q:O_Q + 512].rearrange("p (u c) -> p u c", c=128), sc, R, W)
            for kv, off in ((0, O_KC), (1, O_VC)):
                for g in range(2):
                    for h2 in range(2):
                        cast(wsA[:, 4 + 2 * kv + g, k, h2 * 64:(h2 + 1) * 64], v[:, off + g * 64: off + (g + 1) * 64], sc, R, W)
            cast(wsA[:, 8, k, :], v[:, O_KS:O_KS + 128], sc, R, W)
            cast(wsA[:, 9, k, :], v[:, O_KW:O_KW + 128], sc, R, W)
        flush(["IN_%d" % u for u in range(10)], 10 * 1024)
        wsB = stv[:, 0:8 * 1024].rearrange("p (u k c) -> p u k c", u=8, k=8, c=128)
        o0 = O_VS
        for k in range(8):
            v, key = stage_load(dr["w_in"][0, k * 128:(k + 1) * 128, o0:INW], INW - o0)
            sc = gcol[:, 1, k:k + 1]
            R, W = [key, "gcol"], [("stb", k)]
            cast(wsB[:, 0:4, k, :], v[:, O_XR - o0:O_XR - o0 + 512].rearrange("p (u c) -> p u c", c=128), sc, R, W)
            cast(wsB[:, 4:8, k, :], v[:, O_XG - o0:O_XG - o0 + 512].rearrange("p (u c) -> p u c", c=128), sc, R, W)
            u3, kk = divmod(k, 3)
            base = (8 + u3) * 1024 + kk * 280
            cast(stv[:, base:base + 128], v[:, O_VS - o0:O_VS - o0 + 128], sc, R, W)
            cast(stv[:, base + 128:base + 256], v[:, O_VW - o0:O_VW - o0 + 128], sc, R, W)
            cast(stv[:, base + 256:base + 280], v[:, O_GL - o0:O_GL - o0 + 24], sc, R, W)
        flush(["IN_%d" % u for u in range(10, 21)], 11 * 1024)

    def conv_c1(wname, prefix):
        for hf in range(2):
            v, key = stage_load(dr[wname][0, hf * 1024:(hf + 1) * 1024, :].rearrange("(l p) j -> p l j", p=128), 2048)
            cast(st_bf[:, hf * 2048:(hf + 1) * 2048], v, None, [key], [("stb", hf)])
        flush(["%s_%d" % (prefix, u) for u in range(4)], 4 * 1024)

    def conv_wo():
        for k in range(8):
            v, key = stage_load(dr["w_out"][0, k * 128:(k + 1) * 128, :], 1024)
            kp, kq = divmod(k, 2)
            dst = st_bf[:, 0:8 * 1024].rearrange("p (n kp kq c) -> p n kp kq c", n=2, kp=4, kq=2, c=512)[:, :, kp, kq, :]
            cast(dst, v.rearrange("p (n c) -> p n c", c=512), gocol[:, k:k + 1], [key, "gocol"], [("stb", k)])
        flush(["WO_%d" % u for u in range(8)], 8 * 1024)

    conv_gate_like("ffn1_w_gate", 0, "G1")
    conv_gate_like("ffn1_w_up", 0, "U1")
    conv_down("ffn1_w_down", "D1")
    conv_win()
    conv_c1("cmp_k_w1", "C1K")
    conv_c1("cmp_v_w1", "C1V")
    conv_wo()
    conv_gate_like("ffn2_w_gate", 2, "G2")
    conv_gate_like("ffn2_w_up", 2, "U2")
    conv_down("ffn2_w_down", "D2")

    bar_keys = [("xb", j) for j in range(8)] + [("xbh", j) for j in range(8)] + [("hT", f) for f in range(NF)]
    A("sp", "nop", [], STB + [("stf", 0), ("stf", 1)] + bar_keys)

    def XB(slot):
        return [("xb", slot), ("xbh", slot)]

    def XBH(slot, n):
        return ("xb", slot) if n == 0 else ("xbh", slot)

    recording = wseq is None
    rec = []
    if recording:
        wseq = []
    wpos = [0, 0]

    def w_issue():
        if recording:
            return
        i = wpos[0]
        if i >= len(wseq):
            return
        un = wseq[i]
        slot = i % NS
        DMA("sp", [("wscr", un)], [("ring", slot)], ring[:, slot, :], wscr[UNITS[un]])
        wpos[0] += 1

    def w_next(name):
        if recording:
            rec.append(name)
            return ring[:, 0, :], ("ring", 0)
        i = wpos[1]
        assert wseq[i] == name, (wseq[i], name)
        wpos[1] += 1
        slot = i % NS
        return ring[:, slot, :], ("ring", slot)

    for _ in range(NS):
        w_issue()

    sm_rr = [0]

    def smcol():
        c = sm_rr[0] % 40
        sm_rr[0] += 1
        return small[:, c:c + 1], ("small", c)

    epsb = sb("epsb", [128, 2], F32)
    A("pool", "memset", [], ["epsb"], epsb[:, 0:1], EPS)
    A("pool", "memset", [], ["epsb"], epsb[:, 1:2], 1.0)

    def rstd_from(ssq_ap, ssq_key, n):
        t, tk = smcol()
        A("act", "activation", [ssq_key, "epsb"], [tk], out=t, in_=ssq_ap, func=AF.Sqrt, scale=1.0 / n, bias=epsb[:, 0:1])
        r, rk = smcol()
        A("dve", "reciprocal", [tk], [rk], out=r, in_=t)
        return r, rk

    def prenorm_a(gi):
        slot = gi % 8
        j = gi % 2
        q, qk = smcol()
        A("act", "activation", XB(slot), [("junk", 0), qk], out=junk[:, 0, :], in_=xb[:, slot, :], func=AF.Square, accum_out=q)
        r, rk = rstd_from(q, qk, D)
        A("dve", "tensor_scalar", XB(slot) + [rk], [("xn", j)], out=xn[:, j, :], in0=xb[:, slot, :], scalar1=r, scalar2=None, op0=ALU.mult)

    def prenorm_b(gi, s, b=None):
        j = gi % 2
        if b is None:
            b = 7 - (gi % 2)
        pT = ps[b][:].bitcast(BF16)
        for k in range(8):
            A("pe", "transpose", [("xn", j), "ident"], [("ps", b)], out=pT[:, k * 128:(k + 1) * 128], in_=xn[:, j, k * 128:(k + 1) * 128], identity=ident[:])
        if gi % 2:
            A("act", "activation", [("ps", b)], [("xnT", s)], out=xnT[:, :, s * 128:(s + 1) * 128], in_=pT.rearrange("p (k c) -> p k c", c=128), func=AF.Copy)
        else:
            A("dve", "tensor_copy", [("ps", b)], [("xnT", s)], out=xnT[:, :, s * 128:(s + 1) * 128], in_=pT.rearrange("p (k c) -> p k c", c=128))

    def prenorm_T(gi, s, b=None):
        prenorm_a(gi)
        prenorm_b(gi, s, b)

    XNT_ALL = [("xnT", s) for s in range(4)]

    def ffn(m, pre, gidx, after=None, mid=None, first=None):
        if first is not None:
            first()
        for f in range(NF):
            bg, bu = (0, 1) if f % 2 == 0 else (2, 3)
            wg, wgk = w_next("G%s_%d" % (pre, f))
            for k in range(8):
                A("pe", "matmul", [wgk] + XNT_ALL, [("ps", bg)], ps[bg][:], wg[:, k * 128:(k + 1) * 128], xnT[:, k, :], start=(k == 0), stop=(k == 7))
            w_issue()
            wu, wuk = w_next("U%s_%d" % (pre, f))
            for k in range(8):
                A("pe", "matmul", [wuk] + XNT_ALL, [("ps", bu)], ps[bu][:], wu[:, k * 128:(k + 1) * 128], xnT[:, k, :], start=(k == 0), stop=(k == 7))
            w_issue()
            j = f % 2
            A("act", "activation", [("ps", bg)], [("sg", j)], out=sg[:, j, :], in_=ps[bg][:], func=AF.Silu)
            A("dve", "tensor_tensor", [("sg", j), ("ps", bu)], [("hT", f)], out=hT[:, f, :], in0=sg[:, j, :], in1=ps[bu][:], op=ALU.mult)
        if mid is not None:
            mid()
        for f in range(NF):
            wd, wdk = w_next("D%s_%d" % (pre, f))
            for s in range(4):
                for n in range(2):
                    A("pe", "matmul", [wdk, ("hT", f)], [("ps", 2 * s + n)], ps[2 * s + n][:], hT[:, f, s * 128:(s + 1) * 128], wd[:, n * 512:(n + 1) * 512],
                      start=(f == 0), stop=(f == NF - 1))
            w_issue()
        for s in range(4):
            gi = 4 * m + s
            slot = gi % 8
            q, qk = smcol()
            q2, q2k = smcol()
            for n, (qq, qqk) in enumerate(((q, qk), (q2, q2k))):
                A("act", "activation", [("ps", 2 * s + n)], [("junk", 0), qqk], out=junk[:, 0, n * 512:(n + 1) * 512], in_=ps[2 * s + n][:], func=AF.Square, accum_out=qq)
            t, tk = smcol()
            A("dve", "tensor_tensor", [qk, q2k], [tk], out=t, in0=q, in1=q2, op=ALU.add)
            r, rk = rstd_from(t, tk, D)
            for n in range(2):
                A("dve", "scalar_tensor_tensor", [("ps", 2 * s + n), rk, ("gpost", gidx)], [("tmpf", n)], out=tmpf[:, n, :], in0=ps[2 * s + n][:], scalar=r,
                  in1=gpost[:, gidx, n * 512:(n + 1) * 512], op0=ALU.mult, op1=ALU.mult)
                A("pool" if n == 0 else "dve", "tensor_tensor", [("tmpf", n), XBH(slot, n)], [XBH(slot, n)], out=xb[:, slot, n * 512:(n + 1) * 512], in0=tmpf[:, n, :],
                  in1=xb[:, slot, n * 512:(n + 1) * 512], op=ALU.add)
            if after is not None:
                after(s)

    def load_x(gi):
        if gi >= NT:
            return
        DMA("act", [], XB(gi % 8), xb[:, gi % 8, :], dr["x"][gi * 128:(gi + 1) * 128, :])

    def store_y(gi):
        fin.append(DMA("act", XB(gi % 8), [("y", gi)], y[gi * 128:(gi + 1) * 128, :], xb[:, gi % 8, :]))

    qT = sb("qT", [67, 8, MT], BF16)
    ksT = sb("ksT", [67, 2, T], BF16)
    kwT = sb("kwT", [67, 2, 1024], BF16)
    vs_aug = sb("vs_aug", [128, NT, 2, 65], BF16)
    vw_aug = sb("vw_aug", [128, 8, 2, 65], BF16)
    kc2T = sb("kc2T", [128, 4, 528], BF16)
    kcmpT = sb("kcmpT", [67, 2, NCC * 128], BF16)
    vcmp = sb("vcmp", [128, NCC, 2, 128], BF16)
    expand = sb("expand", [128, T], BF16)
    causalb = sb("causalb", [128, 512], BF16)
    winb = sb("winb", [128, 512], BF16)
    zeros = sb("zeros", [128, 512], BF16)
    fpat = sb("fpat", [128, 3], F32)
    w2 = sb("w2", [128, 2, 2, 64], BF16)
    pe2 = sb("pe2", [128, 2, 16, 2], BF16)
    b1T = sb("b1T", [128, 2, 2], F32)
    wbd = sb("wbd", [128, 2, 4, 128], BF16)
    lcol = sb("lcol", [128, 12, 4], F32)
    hcarry = sb("hcarry", [128, 4], F32)
    xrb = sb("xrb", [128, 4, 515], F32)
    xgb = sb("xgb", [128, 2, 512], F32)
    lruT = sb("lruT", [128, 4, MT], BF16)
    sqacc = sb("sqacc", [128, MT], F32)
    ones_b = sb("ones_b", [128, 2], BF16)
    sqhl = sb("sqhl", [128, 2, MT], BF16)
    PT = sb("PT", [128, 4, 512], BF16)
    selb = sb("selb", [128, 64], BF16)
    selbT = sb("selbT", [128, 4, 128], BF16)
    gates = sb("gates", [128, 4, 24], F32)
    attn = sb("attn", [128, 512], F32)
    attnb = sb("attnb", [128, 512], BF16)
    attnT = sb("attnT", [128, 4, MT], BF16)
    mbuf = tmpf[:].rearrange("p a b -> p (a b)")
    impb = sb("impb", [128, 2, 64], F32)
    m8 = sb("m8", [128, 16], F32)
    hb = sb("hb", [128, 2, 2, 32], BF16)
    vtmp = sb("vtmp", [32, 2, 64], BF16)
    rl_all = sb("rl_all", [128, 4], F32)
    stg = sb("stg", [128, 2, 256], F32)

    hTf = hT[:].rearrange("p a b -> p (a b)").bitcast(F32)

    def wk(j):
        return hTf[:, j * 512:(j + 1) * 512], [("hT", 2 * j), ("hT", 2 * j + 1)]

    DMA("sp", [], ["expand"], expand[:], dr["c_expand"])
    DMA("sp", [], ["causalb"], causalb[:], dr["c_causalb"])
    DMA("sp", [], ["winb"], winb[:], dr["c_winb"])
    DMA("sp", [], ["fpat"], fpat[:], dr["c_fpat"])
    A("pool", "memset", [], ["zeros"], zeros[:], 0.0)
    A("pool", "memset", [], ["selbT"], selbT[:], 0.0)
    A("pool", "memset", [], ["ones_b"], ones_b[:], 1.0)
    A("pool", "memset", [], ["kcmpT"], kcmpT[:], 0.0)
    A("pool", "memset", [], ["vcmp"], vcmp[:], 0.0)
    A("pool", "memset", [], ["kc2T"], kc2T[:], 0.0)
    A("pool", "memset", [], ["xrb"], xrb[:], 0.0)
    A("pool", "memset", [], ["hcarry"], hcarry[:], 0.0)
    A("pool", "memset", [], ["vs_aug"], vs_aug[:], 1.0)
    A("pool", "memset", [], ["vw_aug"], vw_aug[:], 1.0)
    A("pool", "memset", [], ["wbd"], wbd[:], 0.0)
    for g in range(2):
        DMA("sp", [], [("ksT", "aug")], ksT[64:67, g, :], dr["c_kaug"])
        DMA("sp", ["kcmpT"], ["kcmpT"], kcmpT[64:67, g, 0:NCP], dr["c_kcaug"])
        for cc in range(NCC):
            DMA("sp", ["vcmp"], ["vcmp"], vcmp[:, cc, g, 64:128], dr["c_mcs"][cc * 128:(cc + 1) * 128, :])
    for kv, n in enumerate(["cmp_k_w2", "cmp_v_w2"]):
        DMA("sp", [], [("stg", 0)], stg[:, 0, 0:128].rearrange("p (jc d) -> p jc d", d=64), dr[n][0].rearrange("(jc p) d -> p jc d", p=128))
        A("dve", "tensor_copy", [("stg", 0)], ["w2"], out=w2[:, kv, :, :], in_=stg[:, 0, 0:128].rearrange("p (jc d) -> p jc d", d=64))
    pk2 = sb("pk2", [32, 128], F32)
    for kv, n in enumerate(["cmp_k_pe", "cmp_v_pe"]):
        DMA("sp", [], [("pk2", kv)], pk2[16 * kv:16 * kv + 16, :], dr[n][0].rearrange("(lp two) d -> lp (two d)", two=2))
    A("pe", "transpose", [("pk2", 0), ("pk2", 1), "identf"], [("ps", 1)], out=ps[1][:, 0:32], in_=pk2[:], identity=identf[0:32, 0:32])
    for kv in range(2):
        for dup in range(2):
            A("dve", "tensor_copy", [("ps", 1)], ["pe2"], out=pe2[:, kv, :, dup], in_=ps[1][:, 16 * kv:16 * kv + 16])
    for ax, n in enumerate(["lru_w_a", "lru_w_x"]):
        for ch in range(4):
            for hf in range(2):
                DMA("sp", [], [("stg", 0)], stg[hf * 64:(hf + 1) * 64, 0, ch * 64:(ch + 1) * 64], dr[n][0, 2 * ch + hf])
        for ch in range(4):
            for hf in range(2):
                A("dve", "tensor_copy", [("stg", 0)], ["wbd"], out=wbd[hf * 64:(hf + 1) * 64, ax, ch, hf * 64:(hf + 1) * 64],
                  in_=stg[hf * 64:(hf + 1) * 64, 0, ch * 64:(ch + 1) * 64])
    A("dve", "tensor_copy", ["lcol0"], ["lcol"], out=lcol[:, 0:4, :], in_=colT[:, 32:48].rearrange("p (j c) -> p j c", c=4))
    A("dve", "tensor_copy", ["lcol0"], ["lcol"], out=lcol[:, 4:7, :], in_=colT[:, 48:60].rearrange("p (j c) -> p j c", c=4))
    A("dve", "tensor_copy", ["lcol0"], ["lcol"], out=lcol[:, 9, :], in_=colT[:, 60:64])
    A("act", "activation", ["lcol"], ["lcol"], out=lcol[:, 10, :], in_=lcol[:, 9, :], func=AF.Exp, scale=-1.0)
    A("act", "activation", ["lcol", "epsb"], ["lcol"], out=lcol[:, 11, :], in_=lcol[:, 10, :], func=AF.Ln, bias=epsb[:, 1:2])
    A("dve", "tensor_scalar", ["lcol"], ["lcol"], out=lcol[:, 7, :], in0=lcol[:, 11, :], scalar1=-8.0, scalar2=None, op0=ALU.mult)
    A("dve", "tensor_scalar", ["lcol"], ["lcol"], out=lcol[:, 8, :], in0=lcol[:, 11, :], scalar1=-16.0, scalar2=None, op0=ALU.mult)

    C_GELU = 1.5957691216057308

    def gelu_chain(out, x, xk, t1, t1k, n_keys_out):
        A("dve", "tensor_tensor", xk, t1k, out=t1, in0=x, in1=x, op=ALU.mult)
        A("dve", "tensor_scalar", t1k, t1k, out=t1, in0=t1, scalar1=0.044715, scalar2=1.0, op0=ALU.mult, op1=ALU.add)
        A("dve", "tensor_tensor", xk + t1k, t1k, out=t1, in0=t1, in1=x, op=ALU.mult)
        A("act", "activation", t1k, t1k, out=t1, in_=t1, func=AF.Sigmoid, scale=C_GELU)
        A("dve", "tensor_tensor", xk + t1k, n_keys_out, out=out, in0=x, in1=t1, op=ALU.mult)

    def evac(eng, out, in_, reads, writes, scale=None):
        if eng == "act":
            if scale is None:
                A("act", "activation", reads, writes, out=out, in_=in_, func=AF.Copy)
            else:
                A("act", "activation", reads, writes, out=out, in_=in_, func=AF.Copy, scale=scale)
        else:
            if scale is None:
                A(eng, "tensor_copy", reads, writes, out=out, in_=in_)
            else:
                A(eng, "tensor_scalar", reads, writes, out=out, in0=in_, scalar1=scale, scalar2=None, op0=ALU.mult)

    ev_rr = [0]

    def ev_eng():
        ev_rr[0] += 1
        return "act" if ev_rr[0] % 2 else "dve"

    def lru_conv(ch):
        xc, xck = wk(ch)
        xcb, xcbk = hT[:, 16 + ch, :], [("hT", 16 + ch)]
        XR = [("xrb", ch)]
        A("dve", "tensor_scalar", XR + ["lcol"], xck, out=xc, in0=xrb[:, ch, 0:512], scalar1=lcol[:, 0, ch:ch + 1], scalar2=lcol[:, 4, ch:ch + 1],
          op0=ALU.mult, op1=ALU.add)
        for j in range(1, 4):
            A("dve", "scalar_tensor_tensor", XR + ["lcol"] + xck, xck, out=xc, in0=xrb[:, ch, j:j + 512], scalar=lcol[:, j, ch:ch + 1], in1=xc,
              op0=ALU.mult, op1=ALU.add)
        A("pool", "tensor_copy", xck, xcbk, out=xcb, in_=xc)
        A("pool", "tensor_copy", XR, XR, out=xrb[:, ch, 0:3], in_=xrb[:, ch, 512:515])

    def lru_chunk(m, ch, xg_ap, xgk):
        xc, xck = wk(ch)
        xcb, xcbk = hT[:, 16 + ch, :], [("hT", 16 + ch)]
        rr, rrk = wk(4)
        ii, iik = wk(5)
        aa, aak = wk(6)
        mu, muk = wk(7)
        A("pe", "matmul", xcbk + ["wbd"], [("ps", 0)], ps[0][:], wbd[:, 0, ch, :], xcb, start=True, stop=True)
        A("pe", "matmul", xcbk + ["wbd"], [("ps", 1)], ps[1][:], wbd[:, 1, ch, :], xcb, start=True, stop=True)
        A("act", "activation", [("ps", 0), "lcol"], rrk, out=rr, in_=ps[0][:], func=AF.Sigmoid, bias=lcol[:, 5, ch:ch + 1])
        A("act", "activation", [("ps", 1), "lcol"], iik, out=ii, in_=ps[1][:], func=AF.Sigmoid, bias=lcol[:, 6, ch:ch + 1])
        A("act", "activation", rrk + ["lcol"], aak, out=aa, in_=rr, func=AF.Exp, scale=lcol[:, 7, ch:ch + 1])
        A("act", "activation", rrk + ["lcol"], muk, out=mu, in_=rr, func=AF.Exp, scale=lcol[:, 8, ch:ch + 1])
        A("act", "activation", muk + ["epsb"], muk, out=mu, in_=mu, func=AF.Sqrt, scale=-1.0, bias=epsb[:, 1:2])
        A("dve", "tensor_tensor", muk + iik, iik, out=ii, in0=mu, in1=ii, op=ALU.mult)
        A("dve", "tensor_tensor", iik + xck, iik, out=ii, in0=ii, in1=xc, op=ALU.mult)
        A("dve", "tensor_tensor_scan", aak + iik + ["hcarry"], rrk, out=rr, data0=aa, data1=ii, initial=hcarry[:, ch:ch + 1], op0=ALU.mult, op1=ALU.add)
        A("act", "activation", rrk, ["hcarry"], out=hcarry[:, ch:ch + 1], in_=rr[:, 511:512], func=AF.Copy)
        gelu_chain(aa, xg_ap, xgk, mu, muk, aak)
        A("dve", "tensor_tensor", rrk + aak, aak, out=aa, in0=rr, in1=aa, op=ALU.mult)
        A("act", "activation", aak, [("lruT", ch)], out=lruT[:, ch, :], in_=aa, func=AF.Copy)
        if ch == 0:
            A("act", "activation", aak, ["sqacc"], out=sqacc[:], in_=aa, func=AF.Square)
        else:
            A("act", "activation", aak, muk, out=mu, in_=aa, func=AF.Square)
            A("pool", "tensor_tensor", muk + ["sqacc"], ["sqacc"], out=sqacc[:], in0=mu, in1=sqacc[:], op=ALU.add)

    def in_proj(m):
        T0 = m * MT
        DMA("sp", [], [("qT", "aug")], qT[64:67, :, :], dr["c_qaug"][m].rearrange("r (h t) -> r h t", h=8))
        for g in range(2):
            DMA("sp", [("kwT", "aug")], [("kwT", "aug")], kwT[64:67, g, (T0 % 1024):(T0 % 1024) + MT], dr["c_kaug"][:, T0:T0 + MT])
        import os
        BIS = int(os.environ.get("KBIS", "99"))
        bank = [0]

        def skiprest():
            while wpos[1] < len(wseq) and wseq[wpos[1]].startswith("IN_"):
                w_next("IN_")
                w_issue()

        def nb():
            bank[0] = (bank[0] + 1) % 4
            return bank[0]

        for u in range(4):
            w, wkey = w_next("IN_%d" % u)
            wv = w.rearrange("p (k c) -> p k c", c=128)
            for hh in range(2):
                h = 2 * u + hh
                b = nb()
                for k in range(8):
                    A("pe", "matmul", [wkey] + XNT_ALL, [("ps", b)], ps[b][0:64, :], wv[:, k, hh * 64:(hh + 1) * 64], xnT[:, k, :], start=(k == 0), stop=(k == 7))
                evac(ev_eng(), qT[0:64, h, :], ps[b][0:64, :], [("ps", b)], [("qT", h)], scale=0.125)
            w_issue()
        for kvg in range(4):
            w, wkey = w_next("IN_%d" % (4 + kvg))
            wv = w.rearrange("p (k c) -> p k c", c=128)
            b = nb()
            for k in range(8):
                A("pe", "matmul", [wkey] + XNT_ALL, [("ps", b)], ps[b][:], wv[:, k, :], xnT[:, k, :], start=(k == 0), stop=(k == 7))
            w_issue()
            evac("act", kc2T[0:64, kvg, 16:528], ps[b][0:64, :], [("ps", b)], [("kc2T", kvg)])
            evac("dve", kc2T[64:128, kvg, 15:527], ps[b][64:128, :], [("ps", b)], [("kc2T", kvg)])
        for which in range(2):
            w, wkey = w_next("IN_%d" % (8 + which))
            wv = w.rearrange("p (k c) -> p k c", c=128)
            for g in range(2):
                b = nb()
                for k in range(8):
                    A("pe", "matmul", [wkey] + XNT_ALL, [("ps", b)], ps[b][0:64, :], wv[:, k, g * 64:(g + 1) * 64], xnT[:, k, :], start=(k == 0), stop=(k == 7))
                if which == 0:
                    evac(ev_eng(), ksT[0:64, g, T0:T0 + MT], ps[b][0:64, :], [("ps", b)], [("ksT", g, m)])
                else:
                    c0 = T0 % 1024
                    evac(ev_eng(), kwT[0:64, g, c0:c0 + MT], ps[b][0:64, :], [("ps", b)], [("kwT", g, m % 2)])
            w_issue()
        for ch in range(4):
            w, wkey = w_next("IN_%d" % (10 + ch))
            wv = w.rearrange("p (k c) -> p k c", c=128)
            b = nb()
            for k in range(8):
                A("pe", "matmul", [wkey] + XNT_ALL, [("ps", b)], ps[b][:], wv[:, k, :], xnT[:, k, :], start=(k == 0), stop=(k == 7))
            w_issue()
            evac(ev_eng(), xrb[:, ch, 3:515], ps[b][:], [("ps", b)], [("xrb", ch)])
        for ch in range(4):
            lru_conv(ch)
        def tok_unit(u3):
            w, wkey = w_next("IN_%d" % (18 + u3))
            for kk in range(3):
                k = 3 * u3 + kk
                if k >= 8:
                    continue
                for s in range(4):
                    A("pe", "matmul", [wkey, ("xnT", s)], [("ps", 4 + s)], ps[4 + s][:, 0:280], xnT[:, k, s * 128:(s + 1) * 128], w[:, kk * 280:(kk + 1) * 280],
                      start=(k == 0), stop=(k == 7))
            w_issue()

        for ch in range(4):
            w, wkey = w_next("IN_%d" % (14 + ch))
            wv = w.rearrange("p (k c) -> p k c", c=128)
            b = 2 + (ch % 2)
            for k in range(8):
                A("pe", "matmul", [wkey] + XNT_ALL, [("ps", b)], ps[b][:], wv[:, k, :], xnT[:, k, :], start=(k == 0), stop=(k == 7))
            w_issue()
            j = ch % 2
            evac("act", xgb[:, j, :], ps[b][:], [("ps", b)], [("xgb", j)])
            lru_chunk(m, ch, xgb[:, j, :], [("xgb", j)])
            if ch < 3:
                tok_unit(ch)
        for s in range(4):
            i = 4 * m + s
            b = 4 + s
            evac("dve", vs_aug[:, i, :, 0:64], ps[b][:, 0:128].rearrange("p (g d) -> p g d", d=64), [("ps", b)], [("vs", i)])
            evac("act", vw_aug[:, i % 8, :, 0:64], ps[b][:, 128:256].rearrange("p (g d) -> p g d", d=64), [("ps", b)], [("vw", i % 8)])
            A("act", "activation", [("ps", b)], [("gates", s)], out=gates[:, s, :], in_=ps[b][:, 256:280], func=AF.Sigmoid)

    def lru_stats():
        A("pool", "tensor_copy", ["sqacc"], ["sqhi"], out=sqhl[:, 0, :], in_=sqacc[:])
        A("dve", "tensor_tensor", ["sqacc", "sqhi"], ["sqlo"], out=sqhl[:, 1, :], in0=sqacc[:], in1=sqhl[:, 0, :], op=ALU.subtract)
        for s in range(4):
            A("pe", "matmul", ["sqhi", "ones_b"], [("ps", 0)], ps[0][:, 2 * s:2 * s + 2], sqhl[:, 0, s * 128:(s + 1) * 128], ones_b[:], start=True, stop=False)
            A("pe", "matmul", ["sqlo", "ones_b"], [("ps", 0)], ps[0][:, 2 * s:2 * s + 2], sqhl[:, 1, s * 128:(s + 1) * 128], ones_b[:], start=False, stop=True)
        for s in range(4):
            t, tk = smcol()
            A("act", "activation", [("ps", 0), "epsb"], [tk], out=t, in_=ps[0][:, 2 * s:2 * s + 1], func=AF.Sqrt, scale=1.0 / 512, bias=epsb[:, 0:1])
            A("dve", "reciprocal", [tk], [("rl", s)], out=rl_all[:, s:s + 1], in_=t)

    def compress(m):
        for kv in range(2):
            for u in range(4):
                w, wkey = w_next(("C1K_%d" if kv == 0 else "C1V_%d") % u)
                wv = w.rearrange("p (l j) -> p l j", j=256)
                for l4 in range(4):
                    lp = 4 * u + l4
                    for g in range(2):
                        for jc in range(2):
                            b = g * 2 + jc
                            A("pe", "matmul", [wkey, ("kc2T", 2 * kv + g)], [("ps", b)], ps[b][:, 0:32], wv[:, l4, jc * 128:(jc + 1) * 128],
                              kc2T[:, 2 * kv + g, 2 * lp:2 * lp + 497:16], start=(lp == 0), stop=(lp == 15))
                    if m == 0:
                        for jc in range(2):
                            A("pe", "matmul", [wkey, "pe2"], [("ps", 4 + jc)], ps[4 + jc][:, 0:2], wv[:, l4, jc * 128:(jc + 1) * 128], pe2[:, kv, lp, :],
                              start=(lp == 0), stop=(lp == 15))
                w_issue()
            if m == 0:
                for jc in range(2):
                    A("dve", "tensor_copy", [("ps", 4 + jc)], ["b1T"], out=b1T[:, kv, jc:jc + 1], in_=ps[4 + jc][:, 0:1])
            for g in range(2):
                for jc in range(2):
                    b = g * 2 + jc
                    xh = stg[:, 0, (g * 2 + jc) * 32:(g * 2 + jc + 1) * 32]
                    th = stg[:, 1, (g * 2 + jc) * 32:(g * 2 + jc + 1) * 32]
                    xk = [("stgx", g, jc)]
                    tk = [("stgt", g, jc)]
                    A("act", "activation", [("ps", b), "b1T", ("stg", 0)], xk, out=xh, in_=ps[b][:, 0:32], func=AF.Identity, bias=b1T[:, kv, jc:jc + 1])
                    gelu_chain(hb[:, g, jc, :], xh, xk, th, tk + [("stg", 1)], [("hb", g, jc)])
            q4 = m % 4
            cc = m // 4
            for g in range(2):
                if kv == 0:
                    for jc in range(2):
                        A("pe", "matmul", [("hb", g, jc), "w2"], [("ps", 4 + g)], ps[4 + g][0:64, 0:32], w2[:, 0, jc, :], hb[:, g, jc, :], start=(jc == 0), stop=(jc == 1))
                    evac(ev_eng(), kcmpT[0:64, g, 32 * m:32 * m + 32], ps[4 + g][0:64, 0:32], [("ps", 4 + g)], ["kcmpT"])
                else:
                    for jc in range(2):
                        A("pe", "matmul", [("hb", g, jc), "w2"], [("ps", 6 + g)], ps[6 + g][0:32, 0:64], hb[:, g, jc, :], w2[:, 1, jc, :],
                          start=(jc == 0), stop=(jc == 1))
                    evac(ev_eng(), vtmp[:, g, :], ps[6 + g][0:32, 0:64], [("ps", 6 + g)], [("vtmp", g)])
                    if m == 0:
                        A("dve", "memset", [], [("vtmp", g)], vtmp[0:1, g, :], 0.0)
                    DMA("act", [("vtmp", g)], ["vcmp"], vcmp[32 * q4:32 * q4 + 32, cc, g, 0:64], vtmp[:, g, :])
        for kvg in range(4):
            A("pool", "tensor_copy", [("kc2T", kvg)], [("kc2T", kvg)], out=kc2T[:, kvg, 0:16], in_=kc2T[:, kvg, 512:528])
    BIGF = 1e9
    TINY = 1e-30

    def attention(m, s):
        i = 4 * m + s
        for g in range(2):
            qg = qT[0:67, 4 * g:4 * g + 4, s * 128:(s + 1) * 128]
            QK = [("qT", h) for h in range(4 * g, 4 * g + 4)] + [("qT", "aug")]
            ncv = 8 * i + 8
            ncc = (ncv + 127) // 128
            for cc in range(ncc):
                Mc = min(128, ncv - 128 * cc)
                b = cc
                A("pe", "matmul", QK + ["kcmpT"], [("ps", b)], ps[b][0:Mc, :], kcmpT[0:67, g, cc * 128:cc * 128 + Mc], qg, start=True, stop=True)
                A("act", "activation", [("ps", b)], [("PT", cc)], out=PT[0:Mc, cc, :], in_=ps[b][0:Mc, :], func=AF.Exp)
                A("pool", "affine_select", [("PT", cc)], [("PT", cc)], out=PT[0:Mc, cc, :], in_=PT[0:Mc, cc, :], pattern=[[0, 4], [1, 128]],
                  compare_op=ALU.is_ge, fill=0.0, base=128 * i - 2048 * cc - 15, channel_multiplier=-16)
            for h in range(4):
                for cc in range(ncc):
                    Mc = min(128, ncv - 128 * cc)
                    A("pe", "matmul", [("PT", cc), "vcmp"], [("ps", 4)], ps[4][:, h * 128:(h + 1) * 128], PT[0:Mc, cc, h * 128:(h + 1) * 128], vcmp[0:Mc, cc, g, :],
                      start=(cc == 0), stop=(cc == ncc - 1))
            pc = ps[4][:].rearrange("p (h c) -> p h c", c=128)
            l4, l4k = small[:, 40:44], ("sm", "l4")
            r4, r4k = small[:, 44:48], ("sm", "r4")
            A("dve", "tensor_reduce", [("ps", 4)], [l4k], out=l4, in_=pc[:, :, 64:128], axis=AX.X, op=ALU.add)
            A("dve", "tensor_scalar", [l4k], [l4k], out=l4, in0=l4, scalar1=TINY, scalar2=None, op0=ALU.max)
            A("dve", "reciprocal", [l4k], [r4k], out=r4, in_=l4)
            imp = impb[:, 0, :]
            imp2 = impb[:, 1, :]
            A("dve", "tensor_scalar", [("ps", 4), r4k], ["imp"], out=imp, in0=pc[:, 0, 64:128], scalar1=r4[:, 0:1], scalar2=None, op0=ALU.mult)
            for h in range(1, 4):
                A("dve", "scalar_tensor_tensor", [("ps", 4), r4k, "imp"], ["imp"], out=imp, in0=pc[:, h, 64:128], scalar=r4[:, h:h + 1], in1=imp, op0=ALU.mult, op1=ALU.add)
            A("dve", "memset", [], ["imp"], impb[:, 0, 0:1], BIGF)
            lo, hi = max(0, 2 * i - 1), min(64, 2 * i + 2)
            A("dve", "tensor_tensor", ["imp", "fpat"], ["imp"], out=impb[:, 0, lo:hi], in0=impb[:, 0, lo:hi], in1=fpat[:, lo - (2 * i - 1):hi - (2 * i - 1)], op=ALU.max)
            A("dve", "max", ["imp"], ["m8a"], out=m8[:, 0:8], in_=imp)
            A("dve", "match_replace", ["imp", "m8a"], ["imp2"], out=imp2, in_to_replace=m8[:, 0:8], in_values=imp, imm_value=-BIGF)
            A("dve", "max", ["imp2"], ["m8b"], out=m8[:, 8:16], in_=imp2)
            A("dve", "tensor_scalar", ["imp", "m8b"], ["selb"], out=selb[:], in0=imp, scalar1=m8[:, 15:16], scalar2=-BIG, op0=ALU.is_lt, op1=ALU.mult)
            gv = gates[:, s, :].rearrange("p (h b) -> p h b", b=3)
            cf, cfk = small[:, 48:52], ("sm", "cf")
            A("dve", "tensor_tensor", [r4k, ("gates", s)], [cfk], out=cf, in0=r4, in1=gv[:, 4 * g:4 * g + 4, 0], op=ALU.mult)
            for h in range(4):
                hh = 4 * g + h
                A("dve", "tensor_scalar", [("ps", 4), cfk], [("attn", hh)], out=attn[:, hh * 64:(hh + 1) * 64], in0=pc[:, h, 0:64], scalar1=cf[:, h:h + 1], scalar2=None, op0=ALU.mult)
            for br in (1, 0):
                ob = 5 + br
                if br == 0:
                    pT7 = ps[7][:].bitcast(BF16)
                    A("pe", "transpose", ["selb", "ident"], [("ps", 7)], out=pT7[0:64, 0:128], in_=selb[:], identity=ident[:])
                    for h in range(4):
                        evac("act" if h % 2 else "dve", selbT[0:64, h, :], pT7[0:64, 0:128], [("ps", 7)], ["selbT"])
                A("pe", "matmul", ["zeros"], [("ps", ob)], ps[ob][:, 0:260], zeros[:, 0:128], zeros[:, 0:260], start=True, stop=True)
                kcs = list(range(0, i + 1)) if br == 0 else list(range(max(0, i - 4), i + 1))

                def emit_S(n_, kc, br=br, ob=ob):
                    b = n_ % 4
                    j = n_ % 4
                    if br == 0:
                        A("pe", "matmul", QK + [("ksT", g, kc // 4), ("ksT", "aug")], [("ps", b)], ps[b][:], ksT[0:67, g, kc * 128:(kc + 1) * 128], qg, start=True, stop=False)
                        A("pe", "matmul", ["expand", "selbT"], [("ps", b)], ps[b][:], expand[:, kc * 128:(kc + 1) * 128], selbT[:].rearrange("p h q -> p (h q)"),
                          start=False, stop=(kc != i))
                    else:
                        c0 = (kc * 128) % 1024
                        last = not (kc == i or kc == i - 4)
                        A("pe", "matmul", QK + [("kwT", g, (kc // 4) % 2), ("kwT", "aug")], [("ps", b)], ps[b][:], kwT[0:67, g, c0:c0 + 128], qg, start=True, stop=last)
                        if kc == i - 4:
                            A("pe", "matmul", ["ident", "winb"], [("ps", b)], ps[b][:], ident[:], winb[:], start=False, stop=True)
                    if kc == i:
                        A("pe", "matmul", ["ident", "causalb"], [("ps", b)], ps[b][:], ident[:], causalb[:], start=False, stop=True)
                    A("act", "activation", [("ps", b)], [("PT", j)], out=PT[:, j, :], in_=ps[b][:], func=AF.Exp)

                def emit_PV(n_, kc, br=br, ob=ob):
                    j = n_ % 4
                    for h in range(4):
                        if br == 0:
                            A("pe", "matmul", [("PT", j), ("vs", kc)], [("ps", ob)], ps[ob][:, h * 65:(h + 1) * 65], PT[:, j, h * 128:(h + 1) * 128], vs_aug[:, kc, g, :],
                              start=False, stop=(kc == i), skip_group_check=True)
                        else:
                            A("pe", "matmul", [("PT", j), ("vw", kc % 8)], [("ps", ob)], ps[ob][:, h * 65:(h + 1) * 65], PT[:, j, h * 128:(h + 1) * 128], vw_aug[:, kc % 8, g, :],
                              start=False, stop=(kc == i), skip_group_check=True)

                LA = 3
                for n_ in range(len(kcs) + LA):
                    if n_ < len(kcs):
                        emit_S(n_, kcs[n_])
                    if n_ - LA >= 0:
                        emit_PV(n_ - LA, kcs[n_ - LA])
                po = ps[ob][:, 0:260].rearrange("p (h c) -> p h c", c=65)
                lb, lbk = small[:, 52 + 8 * br:56 + 8 * br], ("sm", "lb", br)
                cb, cbk = small[:, 56 + 8 * br:60 + 8 * br], ("sm", "cb", br)
                A("dve", "tensor_scalar", [("ps", ob)], [lbk], out=lb, in0=po[:, :, 64], scalar1=TINY, scalar2=None, op0=ALU.max)
                A("dve", "reciprocal", [lbk], [lbk], out=lb, in_=lb)
                A("dve", "tensor_tensor", [lbk, ("gates", s)], [cbk], out=cb, in0=lb, in1=gv[:, 4 * g:4 * g + 4, 1 + br], op=ALU.mult)
                for h in range(4):
                    hh = 4 * g + h
                    A("dve", "scalar_tensor_tensor", [("ps", ob), cbk, ("attn", hh)], [("attn", hh)], out=attn[:, hh * 64:(hh + 1) * 64], in0=po[:, h, 0:64],
                      scalar=cb[:, h:h + 1], in1=attn[:, hh * 64:(hh + 1) * 64], op0=ALU.mult, op1=ALU.add)
        AK = [("attn", hh) for hh in range(8)]
        q, qk = smcol()
        A("act", "activation", AK, [("junk", 0), qk], out=junk[:, 0, 0:512], in_=attn[:], func=AF.Square, accum_out=q)
        ra, rak = rstd_from(q, qk, 512)
        A("pool", "tensor_copy", AK, ["attnb"], out=attnb[:], in_=attn[:])
        pT7 = ps[7][:].bitcast(BF16)
        for c in range(4):
            A("pe", "transpose", ["attnb", "ident"], [("ps", 7)], out=pT7[:, c * 128:(c + 1) * 128], in_=attnb[:, c * 128:(c + 1) * 128], identity=ident[:])
        evac("dve", attnT[:, :, s * 128:(s + 1) * 128], pT7[:, 0:512].rearrange("p (c q) -> p c q", q=128), [("ps", 7)], [("attnT", s)])
        return ra, rak

    def out_proj_a(m, s, ra, rak):
        gi = 4 * m + s
        slot = gi % 8
        for u in range(8):
            w, wkey = w_next("WO_%d" % u)
            n, kp = divmod(u, 4)
            isl = kp >= 2
            b = 2 * n + (1 if isl else 0)
            for kq in range(2):
                k = 2 * kp + kq
                if isl:
                    lhs, lk = lruT[:, k - 4, s * 128:(s + 1) * 128], [("lruT", k - 4)]
                else:
                    lhs, lk = attnT[:, k, s * 128:(s + 1) * 128], [("attnT", s)]
                A("pe", "matmul", [wkey] + lk, [("ps", b)], ps[b][:], lhs, w[:, kq * 512:(kq + 1) * 512], start=(k % 4 == 0), stop=(k % 4 == 3))
            w_issue()
        for n in range(2):
            A("dve", "tensor_scalar", [("ps", 2 * n), rak], [("tmpf", n)], out=mbuf[:, n * 512:(n + 1) * 512], in0=ps[2 * n][:], scalar1=ra, scalar2=None, op0=ALU.mult)
            A("dve", "scalar_tensor_tensor", [("ps", 2 * n + 1), ("rl", s), ("tmpf", n)], [("tmpf", n)], out=mbuf[:, n * 512:(n + 1) * 512], in0=ps[2 * n + 1][:],
              scalar=rl_all[:, s:s + 1], in1=mbuf[:, n * 512:(n + 1) * 512], op0=ALU.mult, op1=ALU.add)

    def out_proj_b(m, s):
        gi = 4 * m + s
        slot = gi % 8
        q, qk = smcol()
        A("act", "activation", [("tmpf", 0), ("tmpf", 1)], [("junk", 0), qk], out=junk[:, 0, :], in_=mbuf, func=AF.Square, accum_out=q)
        r, rk = rstd_from(q, qk, D)
        A("dve", "scalar_tensor_tensor", [("tmpf", 0), ("tmpf", 1), rk, ("gpost", 1)], [("tmpf", 0), ("tmpf", 1)], out=mbuf, in0=mbuf, scalar=r, in1=gpost[:, 1, :],
          op0=ALU.mult, op1=ALU.mult)
        A("pool", "tensor_tensor", [("tmpf", 0), ("tmpf", 1)] + XB(slot), XB(slot), out=xb[:, slot, :], in0=mbuf, in1=xb[:, slot, :], op=ALU.add)

    for gi in range(8):
        load_x(gi)
    for s in range(4):
        prenorm_T(s, s)
    for m in range(NM):
        ffn(m, "1", 0)
        for s in range(4):
            prenorm_T(4 * m + s, s)
        in_proj(m)
        compress(m)
        lru_stats()
        for s in range(4):
            ra, rak = attention(m, s)
            if s > 0:
                out_proj_b(m, s - 1)
            out_proj_a(m, s, ra, rak)
            if s > 0:
                prenorm_a(4 * m + s - 1)
            if s > 1:
                prenorm_b(4 * m + s - 2, s - 2)
        out_proj_b(m, 3)
        prenorm_a(4 * m + 3)
        prenorm_b(4 * m + 2, 2)
        prenorm_b(4 * m + 3, 3)

        def after2(s, m=m):
            store_y(4 * m + s)
            load_x(4 * (m + 2) + s)

        def first2(m=m):
            if m + 1 < NM:
                prenorm_a(4 * (m + 1))
                prenorm_a(4 * (m + 1) + 1)

        def mid2(m=m):
            if m + 1 < NM:
                g0 = 4 * (m + 1)
                prenorm_b(g0, 0)
                prenorm_b(g0 + 1, 1)
                prenorm_a(g0 + 2)
                prenorm_a(g0 + 3)
                prenorm_b(g0 + 2, 2)
                prenorm_b(g0 + 3, 3)

        ffn(m, "2", 2, after=after2, mid=mid2, first=first2)

    if recording:
        return rec
    with nc.Block() as block:
        semstack = S.emit(nc, block, None, fin)
    return nc, consts


def build_program(T):
    wseq = build(T, stage=9, wseq=None)
    return build(T, stage=9, wseq=wseq)


def kernel(**inputs):
    x = np.ascontiguousarray(np.asarray(inputs["x"], dtype=np.float32))
    B, T, _ = x.shape
    nc, consts = build_program(T)
    in_maps = []
    for b in range(B):
        im = {"x": x[b]}
        for n in WNAMES:
            im[n] = np.ascontiguousarray(np.asarray(inputs[n], dtype=np.float32))
        im.update(consts)
        in_maps.append(im)
    res = run_bass_kernel_spmd(nc, in_maps, core_ids=list(range(B)))
    out = np.stack([np.asarray(r["y"], dtype=np.float32) for r in res.results], axis=0)
    return out
```
